# Optimizing a Trainium2 kernel written in Bass

```python
import jax, jax.numpy as jnp
from jax import lax
import numpy as np

D_MODEL = 1024
BATCH = 8
SEQ = 4096
DEPTH = 1

N_MEM = 256
RMS_EPS = 1e-6
GN_EPS = 1e-5
ATT_HEADS = 8
ATT_HEAD_DIM = 64
ATT_WIDTH = ATT_HEADS * ATT_HEAD_DIM
ROT_DIM = ATT_HEAD_DIM // 4
ROPE_THETA = 500000.0
DILATED_PATTERNS = ((128, 1), (512, 4), (2048, 16))
RET_HEADS = 4
RET_HEAD_DIM = 128
RET_WIDTH = RET_HEADS * RET_HEAD_DIM
RET_CHUNK = 128
RET_ROPE_THETA = 10000.0
MIX_WIDTH = ATT_WIDTH + RET_WIDTH
IN_PROJ_WIDTH = 3 * ATT_WIDTH + 4 * RET_WIDTH
XATT_HEADS = 4
XATT_HEAD_DIM = D_MODEL // XATT_HEADS
FFN_HIDDEN = -(-(8 * D_MODEL) // (3 * 256)) * 256

kernel_name = "hymba_dilated_retention_hybrid_layer"


def rms_norm(x, g):
    xf = x.astype(jnp.float32)
    y = xf * lax.rsqrt(jnp.mean(xf * xf, axis=-1, keepdims=True) + RMS_EPS)
    return (y * g.astype(jnp.float32)).astype(x.dtype)


def rope(x, pos, rot_dim, theta):
    inv = theta ** (-(jnp.arange(0, rot_dim, 2, dtype=jnp.float32) / rot_dim))
    ang = pos.astype(jnp.float32)[:, None] * inv[None, :]
    cos, sin = jnp.cos(ang), jnp.sin(ang)
    xr = x[..., :rot_dim].astype(jnp.float32)
    x1, x2 = xr[..., : rot_dim // 2], xr[..., rot_dim // 2:]
    rot = jnp.concatenate([x1 * cos - x2 * sin, x2 * cos + x1 * sin], axis=-1)
    return jnp.concatenate([rot.astype(x.dtype), x[..., rot_dim:]], axis=-1)


def window_attn_by_stride(q, k, v, window, dilation):
    B, H, S, dh = q.shape
    n_keys = window // dilation
    L = S // dilation
    nb = -(-L // n_keys)
    Lp = nb * n_keys

    def to_classes(t):
        t = t.reshape(B, H, L, dilation, dh).transpose(0, 1, 3, 2, 4)
        return jnp.pad(t, [(0, 0)] * 3 + [(0, Lp - L), (0, 0)])

    def with_prev(t):
        cur = t.reshape(B, H, dilation, nb, n_keys, dh)
        prev = jnp.pad(cur, [(0, 0)] * 3 + [(1, 0), (0, 0), (0, 0)])[:, :, :, :nb]
        return jnp.concatenate([prev, cur], axis=-2)

    qb = to_classes(q).reshape(B, H, dilation, nb, n_keys, dh)
    kb = with_prev(to_classes(k))
    vb = with_prev(to_classes(v)).astype(jnp.float32)
    s = jnp.einsum('bhrnid,bhrnjd->bhrnij', qb, kb).astype(jnp.float32) * (dh ** -0.5)
    i = jnp.arange(n_keys)[:, None]
    j = jnp.arange(2 * n_keys)[None, :]
    band = (j >= i) & (j <= i + n_keys)
    not_before_start = (jnp.arange(nb)[:, None, None] > 0) | (j[None] >= n_keys)
    mask = band[None] & not_before_start
    s = jnp.where(mask, s, -jnp.inf)
    m = jnp.max(s, axis=-1)
    p = jnp.exp(s - m[..., None])
    l = jnp.sum(p, axis=-1)
    o = jnp.einsum('bhrnij,bhrnjd->bhrnid', p, vb) / l[..., None]

    def back(t, has_d):
        if has_d:
            t = t.reshape(B, H, dilation, Lp, dh)[:, :, :, :L]
            return t.transpose(0, 1, 3, 2, 4).reshape(B, H, S, dh)
        t = t.reshape(B, H, dilation, Lp)[:, :, :, :L]
        return t.transpose(0, 1, 3, 2).reshape(B, H, S)

    return back(o, True), back(m, False), back(l, False)


def dilated_attention(q, k, v):
    outs = [window_attn_by_stride(q, k, v, w, r) for (w, r) in DILATED_PATTERNS]
    m_all = jnp.max(jnp.stack([m for _, m, _ in outs], axis=0), axis=0)
    wts = [l * jnp.exp(m - m_all) for _, m, l in outs]
    num = sum(wt[..., None] * o for wt, (o, _, _) in zip(wts, outs))
    den = sum(wts)
    return num / den[..., None]


def retention(q, k, v):
    B, H, S, dk = q.shape
    dv = v.shape[-1]
    C = RET_CHUNK
    nc = S // C
    log_g = jnp.log(1.0 - 2.0 ** (-5.0 - jnp.arange(H, dtype=jnp.float32)))
    n = jnp.arange(C, dtype=jnp.float32)
    rel = n[:, None] - n[None, :]
    decay_mask = jnp.where(rel >= 0, jnp.exp(log_g[:, None, None] * jnp.maximum(rel, 0.0)), 0.0)
    xi = jnp.exp(log_g[:, None] * (n + 1.0))
    zeta = jnp.exp(log_g[:, None] * (C - 1.0 - n))
    chunk_decay = jnp.exp(log_g * C)

    def chunks(t):
        return t.reshape(B, H, nc, C, t.shape[-1]).transpose(2, 0, 1, 3, 4)

    def step(R, qkv):
        qc, kc, vc = qkv
        inner = jnp.einsum('bhnd,bhmd->bhnm', qc, kc) * decay_mask
        o = jnp.einsum('bhnm,bhme->bhne', inner, vc)
        o = o + jnp.einsum('bhnd,bhde->bhne', qc, R) * xi[:, :, None]
        R = R * chunk_decay[:, None, None] + jnp.einsum('bhmd,bhme->bhde', kc * zeta[:, :, None], vc)
        return R, o

    R0 = jnp.zeros((B, H, dk, dv), jnp.float32)
    _, o = lax.scan(step, R0, (chunks(q), chunks(k), chunks(v)))
    return o.transpose(1, 2, 0, 3, 4).reshape(B, H, S, dv)


def hybrid_mixer(h, w_in, attn_gn_g, ret_gn_g, w_out):
    B, S, _ = h.shape
    proj = h @ w_in
    splits = list(np.cumsum([ATT_WIDTH] * 3 + [RET_WIDTH] * 3))
    aq, ak, av, rq, rk, rv, rg = jnp.split(proj, splits, axis=-1)
    pos = jnp.arange(S)

    def heads(t, nh, d):
        return t.reshape(B, S, nh, d).transpose(0, 2, 1, 3)

    aq = rope(heads(aq, ATT_HEADS, ATT_HEAD_DIM), pos, ROT_DIM, ROPE_THETA)
    ak = rope(heads(ak, ATT_HEADS, ATT_HEAD_DIM), pos, ROT_DIM, ROPE_THETA)
    a = dilated_attention(aq, ak, heads(av, ATT_HEADS, ATT_HEAD_DIM))
    a = a * lax.rsqrt(jnp.mean(a * a, axis=-1, keepdims=True) + RMS_EPS)
    a = a.transpose(0, 2, 1, 3).reshape(B, S, ATT_WIDTH) * attn_gn_g.astype(jnp.float32)

    rq = rope(heads(rq, RET_HEADS, RET_HEAD_DIM).astype(jnp.float32), pos, RET_HEAD_DIM, RET_ROPE_THETA)
    rk = rope(heads(rk, RET_HEADS, RET_HEAD_DIM).astype(jnp.float32), pos, RET_HEAD_DIM, RET_ROPE_THETA)
    rk = rk * (RET_HEAD_DIM ** -0.5)
    r = retention(rq, rk, heads(rv, RET_HEADS, RET_HEAD_DIM).astype(jnp.float32))
    mu = jnp.mean(r, axis=-1, keepdims=True)
    var = jnp.mean(jnp.square(r - mu), axis=-1, keepdims=True)
    r = (r - mu) * lax.rsqrt(var + GN_EPS)
    r = r.transpose(0, 2, 1, 3).reshape(B, S, RET_WIDTH) * ret_gn_g.astype(jnp.float32)
    r = jax.nn.silu(rg.astype(jnp.float32)) * r

    y = jnp.concatenate([a, r], axis=-1).astype(h.dtype)
    return y @ w_out


def memory_cross_attn(h, mem, mem_norm_g, w_q, w_kv, w_o):
    B, S, _ = h.shape
    mh = rms_norm(mem, mem_norm_g)
    q = (h @ w_q).reshape(B, S, XATT_HEADS, XATT_HEAD_DIM)
    k, v = jnp.split(mh @ w_kv, 2, axis=-1)
    k = k.reshape(B, N_MEM, XATT_HEADS, XATT_HEAD_DIM)
    v = v.reshape(B, N_MEM, XATT_HEADS, XATT_HEAD_DIM)
    s = jnp.einsum('bshd,bmhd->bhsm', q, k).astype(jnp.float32) * (XATT_HEAD_DIM ** -0.5)
    p = jax.nn.softmax(s, axis=-1)
    o = jnp.einsum('bhsm,bmhd->bshd', p.astype(v.dtype), v).reshape(B, S, D_MODEL)
    return o @ w_o


def swiglu(h, w_gate_up, w_down):
    g, u = jnp.split(h @ w_gate_up, 2, axis=-1)
    return (jax.nn.silu(g) * u) @ w_down


def setup_inputs(seed: int = 0) -> dict:
    key = jax.random.key(seed)
    ks = jax.random.split(key, 20)

    def w(k, shape, fan_in):
        return jax.random.normal(k, shape, jnp.float32) * (fan_in ** -0.5)

    def gain(k, n):
        return 1.0 + 0.05 * jax.random.normal(k, (DEPTH, n), jnp.float32)

    return {
        "x": jax.random.normal(ks[0], (BATCH, SEQ, D_MODEL), jnp.float32),
        "mem": jax.random.normal(ks[1], (BATCH, N_MEM, D_MODEL), jnp.float32),
        "pre_mix_g": gain(ks[2], D_MODEL),
        "post_mix_g": gain(ks[3], D_MODEL),
        "w_in": w(ks[4], (DEPTH, D_MODEL, IN_PROJ_WIDTH), D_MODEL),
        "attn_gn_g": gain(ks[5], ATT_WIDTH),
        "ret_gn_g": gain(ks[6], RET_WIDTH),
        "w_out": w(ks[7], (DEPTH, MIX_WIDTH, D_MODEL), MIX_WIDTH),
        "pre_mem_g": gain(ks[8], D_MODEL),
        "post_mem_g": gain(ks[9], D_MODEL),
        "mem_norm_g": gain(ks[10], D_MODEL),
        "w_q_mem": w(ks[11], (DEPTH, D_MODEL, D_MODEL), D_MODEL),
        "w_kv_mem": w(ks[12], (DEPTH, D_MODEL, 2 * D_MODEL), D_MODEL),
        "w_o_mem": w(ks[13], (DEPTH, D_MODEL, D_MODEL), D_MODEL),
        "pre_ffn_g": gain(ks[14], D_MODEL),
        "post_ffn_g": gain(ks[15], D_MODEL),
        "w_gate_up": w(ks[16], (DEPTH, D_MODEL, 2 * FFN_HIDDEN), D_MODEL),
        "w_down": w(ks[17], (DEPTH, FFN_HIDDEN, D_MODEL), FFN_HIDDEN),
    }


def reference(x, mem, pre_mix_g, post_mix_g, w_in, attn_gn_g, ret_gn_g, w_out,
              pre_mem_g, post_mem_g, mem_norm_g, w_q_mem, w_kv_mem, w_o_mem,
              pre_ffn_g, post_ffn_g, w_gate_up, w_down):
    for l in range(DEPTH):
        y = hybrid_mixer(rms_norm(x, pre_mix_g[l]), w_in[l], attn_gn_g[l], ret_gn_g[l], w_out[l])
        x = x + rms_norm(y, post_mix_g[l])
        y = memory_cross_attn(rms_norm(x, pre_mem_g[l]), mem, mem_norm_g[l],
                              w_q_mem[l], w_kv_mem[l], w_o_mem[l])
        x = x + rms_norm(y, post_mem_g[l])
        y = swiglu(rms_norm(x, pre_ffn_g[l]), w_gate_up[l], w_down[l])
        x = x + rms_norm(y, post_ffn_g[l])
    return x
```

```python
import contextlib
import os
import numpy as np
import concourse.bass as bass
import concourse.mybir as mybir
from concourse.bass_utils import run_bass_kernel_spmd

F32 = mybir.dt.float32
BF16 = mybir.dt.bfloat16
AF = mybir.ActivationFunctionType
ALU = mybir.AluOpType

S = 4096
D = 1024
NT = 32
RMS_EPS = 1e-6
GN_EPS = 1e-5
FFN = 2816
NFC = 22


class Prog:
    ENG = ('pe', 'act', 'dve', 'pool', 'sp')
    LIM = 16000

    def __init__(self, nc, es):
        self.nc = nc
        self.es = es
        self.esem = {e: [] for e in ('pe', 'act', 'dve', 'pool')}
        self.dsem = {}
        self.sigcount = {e: 0 for e in ('pe', 'act', 'dve', 'pool')}
        self.dma_cnt = {}
        self.begin_pass()

    def begin_pass(self):
        self.q = {e: [] for e in self.ENG}
        self.last_w = {}
        self.readers = {}
        self.waited = {e: {} for e in self.ENG}
        self.pass_dma = {}

    def _dep(self, ins, sig):
        if sig is None:
            return
        qn = ins['q']
        kind, ref, val = sig
        if kind == 'eng' and ref == 'pe' and qn == 'pe':
            return
        if self.waited[qn].get((kind, ref), -1) >= val:
            return
        self.waited[qn][(kind, ref)] = val
        ins['waits'].append(sig)
        if kind == 'eng':
            self.q[ref][val]['signal'] = True

    def add(self, qn, fn, reads=(), writes=(), dma_slot=None, track=True):
        ins = {'q': qn, 'fn': fn, 'waits': [], 'signal': False, 'dma_slot': dma_slot}
        if dma_slot is None:
            sig = ('eng', qn, len(self.q[qn]))
        else:
            c = self.dma_cnt.get(dma_slot, 0) + 1
            assert c < 2000, dma_slot
            self.dma_cnt[dma_slot] = c
            sig = ('dma', dma_slot, 16 * c)
            self.pass_dma[dma_slot] = 16 * c
        ins['sig'] = sig
        if track:
            for k in reads:
                self._dep(ins, self.last_w.get(k))
                if k.startswith('ps'):
                    for r in self.readers.get(k, ()):
                        if not (r[0] == 'eng' and r[1] == qn):
                            self._dep(ins, r)
            for k in writes:
                self._dep(ins, self.last_w.get(k))
                for r in self.readers.get(k, ()):
                    self._dep(ins, r)
        for k in reads:
            self.readers.setdefault(k, []).append(sig)
        for k in writes:
            self.last_w[k] = sig
            self.readers[k] = []
        self.q[qn].append(ins)
        return sig

    def _esem(self, e, idx):
        while len(self.esem[e]) <= idx:
            self.esem[e].append(self.es.enter_context(self.nc.semaphore('s_%s%d' % (e, len(self.esem[e])))))
        return self.esem[e][idx]

    def _dsem(self, slot):
        if slot not in self.dsem:
            self.dsem[slot] = self.es.enter_context(self.nc.semaphore('d_%d' % len(self.dsem)))
        return self.dsem[slot]

    def emit(self):
        nc = self.nc
        for e in self.esem:
            c = self.sigcount[e]
            for ins in self.q[e]:
                if ins['dma_slot'] is None and ins['signal']:
                    ins['semidx'] = c // self.LIM
                    ins['sigval'] = c % self.LIM + 1
                    c += 1
            self.sigcount[e] = c

        def resolve(sig):
            kind, ref, val = sig
            if kind == 'eng':
                i = self.q[ref][val]
                return self._esem(ref, i['semidx']), i['sigval']
            return self._dsem(ref), val

        final = [('dma', s, v) for s, v in self.pass_dma.items()]
        for e in self.ENG:
            for ins in self.q[e]:
                for w in ins['waits']:
                    resolve(w)
                if ins['dma_slot'] is not None:
                    self._dsem(ins['dma_slot'])
                elif ins['signal']:
                    self._esem(e, ins['semidx'])

        def run(e, eng):
            for ins in self.q[e]:
                for w in ins['waits']:
                    s, v = resolve(w)
                    eng.wait_ge(s, v)
                bi = ins['fn'](eng)
                if ins['dma_slot'] is not None:
                    bi.then_inc(self._dsem(ins['dma_slot']), 16)
                elif ins['signal']:
                    bi.then_inc(self._esem(e, ins['semidx']), 1)
            if e == 'sp':
                for sg in final:
                    s, v = resolve(sg)
                    eng.wait_ge(s, v)

        with nc.Block() as block:
            @block.tensor
            def _(eng):
                run('pe', eng)

            @block.scalar
            def _(eng):
                run('act', eng)

            @block.vector
            def _(eng):
                run('dve', eng)

            @block.gpsimd
            def _(eng):
                run('pool', eng)

            @block.sync
            def _(eng):
                run('sp', eng)

    def mm(self, out, lhsT, rhs, start, stop, reads, writes):
        return self.add('pe', lambda e: e.matmul(out, lhsT=lhsT, rhs=rhs, start=start, stop=stop), reads, writes)

    def tr(self, out, in_, ident, reads, writes):
        return self.add('pe', lambda e: e.transpose(out=out, in_=in_, identity=ident), reads, writes)

    def act(self, out, in_, func, reads, writes, scale=None, accum=None):
        def f(e):
            kw = {}
            if scale is not None:
                kw['scale'] = scale
            if accum is not None:
                kw['accum_out'] = accum
            return e.activation(out=out, in_=in_, func=func, **kw)
        return self.add('act', f, reads, writes)

    def tt(self, q, out, in0, in1, op, reads, writes):
        return self.add(q, lambda e: e.tensor_tensor(out=out, in0=in0, in1=in1, op=op), reads, writes)

    def ts(self, q, out, in0, s1, s2, op0, op1, reads, writes):
        if s2 is None:
            return self.add(q, lambda e: e.tensor_scalar(out=out, in0=in0, scalar1=s1, scalar2=None, op0=op0), reads, writes)
        return self.add(q, lambda e: e.tensor_scalar(out=out, in0=in0, scalar1=s1, scalar2=s2, op0=op0, op1=op1), reads, writes)

    def stt(self, out, in0, scalar, in1, op0, op1, reads, writes):
        return self.add('dve', lambda e: e.scalar_tensor_tensor(out=out, in0=in0, scalar=scalar, in1=in1, op0=op0, op1=op1), reads, writes)

    def cp(self, q, out, in_, reads, writes):
        if q == 'act':
            return self.add('act', lambda e: e.copy(out=out, in_=in_), reads, writes)
        return self.add(q, lambda e: e.tensor_copy(out=out, in_=in_), reads, writes)

    def recip(self, out, in_, reads, writes):
        return self.add('dve', lambda e: e.reciprocal(out=out, in_=in_), reads, writes)

    def dma(self, q, out, in_, slot, reads=(), writes=(), track=True):
        return self.add(q, lambda e: e.dma_start(out=out, in_=in_), reads, writes, dma_slot=slot, track=track)

    def memset(self, q, ap, val, writes):
        return self.add(q, lambda e: e.memset(ap, val), (), writes)

    def rstd(self, ss_ap, tmp_ap, out_ap, mult, eps, key):
        self.ts('dve', tmp_ap, ss_ap, mult, eps, ALU.mult, ALU.add, [key], [key + '_t'])
        self.act(tmp_ap, tmp_ap, AF.Sqrt, [key + '_t'], [key + '_t'])
        self.recip(out_ap, tmp_ap, [key + '_t'], [key + '_r'])


def load_w(P, dst, src, nchunk, key):
    for dc in range(nchunk):
        P.dma('pool', dst[:, dc, :], src[dc * 128:(dc + 1) * 128, :], slot=key, writes=[key], track=False)


def pass1(P, nc, T_, dbg_tiles=NT):
    x, w_in = T_['x'], T_['w_in']
    with contextlib.ExitStack() as es:
        sb = lambda n, s, d: es.enter_context(nc.sbuf_tensor(n, s, d))
        ps = lambda n, s, d: es.enter_context(nc.psum_tensor(n, s, d))
        P.begin_pass()
        W = sb('p1_w', [128, 8, 3584], BF16)
        ident = sb('p1_id', [128, 128], BF16)
        cosR = sb('p1_cosR', [128, 32, 64], F32)
        sinR = sb('p1_sinR', [128, 32, 64], F32)
        cosA = sb('p1_cosA', [128, 32, 8], F32)
        sinA = sb('p1_sinA', [128, 32, 8], F32)
        decT = sb('p1_decT', [128, 4, 128], F32)
        xiT = sb('p1_xiT', [128, 4, 128], F32)
        zet = sb('p1_zet', [128, 4, 128], F32)
        gpre = sb('p1_gpre', [128, 1024], F32)
        gret = sb('p1_gret', [128, 512], F32)
        xt = [sb('p1_xt%d' % i, [128, 1024], F32) for i in range(2)]
        junk = sb('p1_junk', [128, 1024], BF16)
        st = sb('p1_st', [128, 16], F32)
        hb = [sb('p1_hb%d' % i, [128, 1024], BF16) for i in range(2)]
        hTb = [sb('p1_hT%d' % i, [128, 8, 512], BF16) for i in range(2)]
        tA = sb('p1_tA', [128, 4, 64], F32)
        qka = sb('p1_qka', [128, 2, 512], BF16)
        aqkT = [sb('p1_aqkT%d' % i, [128, 2, 4, 512], BF16) for i in range(2)]
        avT = [sb('p1_avT%d' % i, [128, 4, 512], BF16) for i in range(2)]
        yTr = [sb('p1_yTr%d' % i, [128, 4, 512], BF16) for i in range(2)]
        tR = sb('p1_tR', [128, 4, 256], F32)
        qkr = sb('p1_qkr', [128, 2, 512], BF16)
        kz = sb('p1_kz', [128, 4, 128], BF16)
        QKT = [sb('p1_QKT%d' % i, [128, 8, 128], BF16) for i in range(2)]
        QTxi = [sb('p1_QTxi%d' % i, [128, 4, 128], BF16) for i in range(2)]
        vb = [sb('p1_vb%d' % i, [128, 512], BF16) for i in range(2)]
        gs = sb('p1_gs', [128, 512], F32)
        gs2 = sb('p1_gs2', [128, 512], F32)
        innT = sb('p1_innT', [128, 4, 128], BF16)
        R = sb('p1_R', [128, 4, 128], F32)
        Rbf = sb('p1_Rbf', [128, 4, 128], BF16)
        osb = sb('p1_osb', [128, 512], F32)
        rn = sb('p1_rn', [128, 512], F32)
        rb = sb('p1_rb', [128, 512], BF16)
        gst = sb('p1_gst', [128, 32], F32)
        psQ = ps('p1_psQ', [128, 512], F32)
        psK = ps('p1_psK', [128, 512], F32)
        psVG = ps('p1_psVG', [128, 512], F32)
        psT = ps('p1_psT', [128, 8, 128], BF16)
        psS = ps('p1_psS', [128, 512], F32)
        psO = ps('p1_psO', [128, 512], F32)
        psR = ps('p1_psR', [128, 512], F32)
        psAV = ps('p1_psAV', [128, 512], F32)

        load_w(P, W, w_in, 8, 'W')
        P.dma('pool', ident[:], T_['ident'][:, :], 'ident', writes=['ident'], track=False)
        for nm, t in (('cosR', cosR), ('sinR', sinR), ('cosA', cosA), ('sinA', sinA), ('decT', decT), ('xiT', xiT), ('zet', zet)):
            P.dma('sp', t[:], T_[nm][:, :, :], nm, writes=[nm], track=False)
        P.dma('sp', gpre[:], T_['gt'][:, 0, :], 'gpre', writes=['gpre'], track=False)
        P.dma('sp', gret[:], T_['gt'][:, 9, 0:512], 'gret', writes=['gret'], track=False)
        P.memset('pool', R[:], 0.0, ['R'])

        cd = T_['cd']
        for blk in range(dbg_tiles // 4):
            bi = blk % 2
            for t in range(4):
                T = 4 * blk + t
                i = T % 2
                tsl = slice(t * 128, (t + 1) * 128)
                P.dma('sp', xt[i][:], x[T * 128:(T + 1) * 128, :], 'xt%d' % i, writes=['xt%d' % i])
                P.act(junk[:], xt[i][:], AF.Square, ['xt%d' % i], ['junk', 'ss'], accum=st[:, 0:1])
                P.rstd(st[:, 0:1], st[:, 1:2], st[:, 2:3], 1.0 / D, RMS_EPS, 'ss')
                P.stt(hb[i][:], xt[i][:], st[:, 2:3], gpre[:], ALU.mult, ALU.mult, ['xt%d' % i, 'ss_r', 'gpre'], ['hb%d' % i])
                for dc in range(8):
                    P.tr(psT[:, dc, :], hb[i][:, dc * 128:(dc + 1) * 128], ident[:], ['hb%d' % i, 'ident'], ['psT'])
                P.cp('act', hTb[bi][:, :, tsl], psT[:], ['psT'], ['hT%d_%d' % (bi, t)])
                hk = 'hT%d_%d' % (bi, t)

                def proj(psum, c0, key):
                    for dc in range(8):
                        P.mm(psum[:], hTb[bi][:, dc, tsl], W[:, dc, c0:c0 + 512], dc == 0, dc == 7, [hk, 'W'], [key])

                proj(psQ, 0, 'psQ')
                proj(psK, 512, 'psK')
                cb = cosA[:, T, :].unsqueeze(1).broadcast_to([128, 8, 8])
                snb = sinA[:, T, :].unsqueeze(1).broadcast_to([128, 8, 8])
                for j, (pp, pk) in enumerate(((psQ, 'psQ'), (psK, 'psK'))):
                    v = pp[:].rearrange("p (h d) -> p h d", h=8)
                    x1, x2 = v[:, :, 0:8], v[:, :, 8:16]
                    tv = [tA[:, k, :].rearrange("p (h d) -> p h d", h=8) for k in range(4)]
                    P.tt('dve', tv[0], x1, cb, ALU.mult, [pk, 'cosA'], ['tA0'])
                    P.tt('dve', tv[1], x2, snb, ALU.mult, [pk, 'sinA'], ['tA1'])
                    P.tt('dve', tv[2], x2, cb, ALU.mult, [pk, 'cosA'], ['tA2'])
                    P.tt('dve', tv[3], x1, snb, ALU.mult, [pk, 'sinA'], ['tA3'])
                    o = qka[:, j, :].rearrange("p (h d) -> p h d", h=8)
                    P.tt('pool', o[:, :, 0:8], tv[0], tv[1], ALU.subtract, ['tA0', 'tA1'], ['qka'])
                    P.tt('pool', o[:, :, 8:16], tv[2], tv[3], ALU.add, ['tA2', 'tA3'], ['qka'])
                    P.cp('act', o[:, :, 16:64], v[:, :, 16:64], [pk], ['qka'])
                for j in range(2):
                    for pr in range(4):
                        P.tr(psT[:, j * 4 + pr, :], qka[:, j, pr * 128:(pr + 1) * 128], ident[:], ['qka', 'ident'], ['psT'])
                P.cp('act', aqkT[bi][:, :, :, tsl], psT[:].rearrange("p (a b) c -> p a b c", a=2), ['psT'], ['aqkT%d' % bi])

                proj(psQ, 1536, 'psQ')
                proj(psK, 2048, 'psK')
                cb = cosR[:, T, :].unsqueeze(1).broadcast_to([128, 4, 64])
                snb = sinR[:, T, :].unsqueeze(1).broadcast_to([128, 4, 64])
                for j, (pp, pk) in enumerate(((psQ, 'psQ'), (psK, 'psK'))):
                    v = pp[:].rearrange("p (h d) -> p h d", h=4)
                    x1, x2 = v[:, :, 0:64], v[:, :, 64:128]
                    tv = [tR[:, k, :].rearrange("p (h d) -> p h d", h=4) for k in range(4)]
                    P.tt('dve', tv[0], x1, cb, ALU.mult, [pk, 'cosR'], ['tR0'])
                    P.tt('dve', tv[1], x2, snb, ALU.mult, [pk, 'sinR'], ['tR1'])
                    P.tt('dve', tv[2], x2, cb, ALU.mult, [pk, 'cosR'], ['tR2'])
                    P.tt('dve', tv[3], x1, snb, ALU.mult, [pk, 'sinR'], ['tR3'])
                    o = qkr[:, j, :].rearrange("p (h d) -> p h d", h=4)
                    P.tt('pool', o[:, :, 0:64], tv[0], tv[1], ALU.subtract, ['tR0', 'tR1'], ['qkr'])
                    P.tt('pool', o[:, :, 64:128], tv[2], tv[3], ALU.add, ['tR2', 'tR3'], ['qkr'])
                P.tt('pool', kz[:], qkr[:, 1, :].rearrange("p (h d) -> p h d", h=4), zet[:], ALU.mult, ['qkr', 'zet'], ['kz'])
                for j in range(2):
                    for h in range(4):
                        P.tr(psT[:, j * 4 + h, :], qkr[:, j, h * 128:(h + 1) * 128], ident[:], ['qkr', 'ident'], ['psT'])
                P.cp('act', QKT[i][:], psT[:], ['psT'], ['QKT%d' % i])
                P.tt('dve', QTxi[i][:], psT[:, 0:4, :], xiT[:], ALU.mult, ['psT', 'xiT'], ['QTxi%d' % i])

                proj(psVG, 2560, 'psVG')
                P.cp('act', vb[i][:], psVG[:], ['psVG'], ['vb%d' % i])
                proj(psVG, 3072, 'psVG')
                P.act(gs[:], psVG[:], AF.Silu, ['psVG'], ['gs'])
                P.tt('pool', gs2[:], gs[:], gret[:], ALU.mult, ['gs', 'gret'], ['gs2'])

                for h in range(4):
                    hs = slice(h * 128, (h + 1) * 128)
                    P.mm(psS[:, hs], QKT[i][:, 4 + h, :], QKT[i][:, h, :], True, True, ['QKT%d' % i], ['psS'])
                P.tt('dve', innT[:], psS[:].rearrange("p (h n) -> p h n", h=4), decT[:], ALU.mult, ['psS', 'decT'], ['innT'])
                for h in range(4):
                    hs = slice(h * 128, (h + 1) * 128)
                    P.mm(psO[:, hs], innT[:, h, :], vb[i][:, hs], True, T == 0, ['innT', 'vb%d' % i], ['psO'])
                    if T > 0:
                        P.mm(psO[:, hs], QTxi[i][:, h, :], Rbf[:, h, :], False, True, ['QTxi%d' % i, 'Rbf'], ['psO'])
                if T < NT - 1:
                    for h in range(4):
                        hs = slice(h * 128, (h + 1) * 128)
                        P.mm(psR[:, hs], kz[:, h, :], vb[i][:, hs], True, True, ['kz', 'vb%d' % i], ['psR'])
                    for h in range(4):
                        hs = slice(h * 128, (h + 1) * 128)
                        P.stt(R[:, h, :], R[:, h, :], float(cd[h]), psR[:, hs], ALU.mult, ALU.add, ['R', 'psR'], ['R'])
                    P.cp('act', Rbf[:], R[:], ['R'], ['Rbf'])
                for h in range(4):
                    hs = slice(h * 128, (h + 1) * 128)
                    P.act(osb[:, hs], psO[:, hs], AF.Copy, ['psO'], ['osb', 'gst_s'], accum=gst[:, h:h + 1])
                for h in range(4):
                    hs = slice(h * 128, (h + 1) * 128)
                    P.act(junk[:, hs], psO[:, hs], AF.Square, ['psO'], ['junk', 'gst_q'], accum=gst[:, 4 + h:5 + h])
                P.ts('dve', gst[:, 8:12], gst[:, 0:4], 1.0 / 128, None, ALU.mult, None, ['gst_s'], ['gst_m'])
                P.tt('dve', gst[:, 12:16], gst[:, 8:12], gst[:, 8:12], ALU.mult, ['gst_m'], ['gst_m2'])
                P.stt(gst[:, 16:20], gst[:, 4:8], 1.0 / 128, gst[:, 12:16], ALU.mult, ALU.subtract, ['gst_q', 'gst_m2'], ['gst_v'])
                P.ts('dve', gst[:, 20:24], gst[:, 16:20], GN_EPS, None, ALU.add, None, ['gst_v'], ['gst_ve'])
                P.act(gst[:, 20:24], gst[:, 20:24], AF.Sqrt, ['gst_ve'], ['gst_ve'])
                P.recip(gst[:, 24:28], gst[:, 20:24], ['gst_ve'], ['gst_r'])
                for h in range(4):
                    hs = slice(h * 128, (h + 1) * 128)
                    P.ts('dve', rn[:, hs], osb[:, hs], gst[:, 8 + h:9 + h], gst[:, 24 + h:25 + h], ALU.subtract, ALU.mult,
                         ['osb', 'gst_m', 'gst_r'], ['rn'])
                P.tt('pool', rb[:], rn[:], gs2[:], ALU.mult, ['rn', 'gs2'], ['rb'])
                for h in range(4):
                    P.tr(psT[:, h, :], rb[:, h * 128:(h + 1) * 128], ident[:], ['rb', 'ident'], ['psT'])
                P.cp('act', yTr[bi][:, :, tsl], psT[:, 0:4, :], ['psT'], ['yTr%d' % bi])

            hks = ['hT%d_%d' % (bi, t) for t in range(4)]
            for pr in range(4):
                for dc in range(8):
                    P.mm(psAV[:], W[:, dc, 1024 + pr * 128:1024 + (pr + 1) * 128], hTb[bi][:, dc, :], dc == 0, dc == 7, hks + ['W'], ['psAV'])
                P.cp('act', avT[bi][:, pr, :], psAV[:], ['psAV'], ['avT%d' % bi])
            bs = slice(blk * 512, (blk + 1) * 512)
            P.dma('sp', T_['QTd'][:, :, bs].rearrange("a p n -> p a n"), aqkT[bi][:, 0, :, :], 'oq%d' % bi, reads=['aqkT%d' % bi])
            P.dma('sp', T_['KTd'][:, :, bs].rearrange("a p n -> p a n"), aqkT[bi][:, 1, :, :], 'ok%d' % bi, reads=['aqkT%d' % bi])
            P.dma('sp', T_['VTd'][:, :, bs].rearrange("a p n -> p a n"), avT[bi][:], 'ov%d' % bi, reads=['avT%d' % bi])
            P.dma('sp', T_['YTd'][4:8, :, bs].rearrange("a p n -> p a n"), yTr[bi][:], 'oy%d' % bi, reads=['yTr%d' % bi])
        P.emit()


def pass2(P, nc, T_, npairs=4):
    with contextlib.ExitStack() as es:
        sb = lambda n, s, d: es.enter_context(nc.sbuf_tensor(n, s, d))
        ps = lambda n, s, d: es.enter_context(nc.psum_tensor(n, s, d))
        P.begin_pass()
        ident = sb('p2_id', [128, 128], BF16)
        mask = sb('p2_mask', [128, 512], BF16)
        ones_bd = sb('p2_onesbd', [128, 128], BF16)
        gA = sb('p2_gA', [128, 4], F32)
        QT = sb('p2_QT', [128, S], BF16)
        KTA = sb('p2_KTA', [128, S], BF16)
        KTB = sb('p2_KTB', [128, S], BF16)
        VT = sb('p2_VT', [128, S], BF16)
        ACC = [sb('p2_ACC%d' % i, [128, S], F32) for i in range(2)]
        DEN = sb('p2_DEN', [128, S], F32)
        yTp = sb('p2_yTp', [128, S], BF16)
        NPT = 4
        PT = [sb('p2_PT%d' % i, [128, 512], BF16) for i in range(NPT)]
        VB = [sb('p2_VB%d' % i, [128, 256], BF16) for i in range(NPT)]
        sq = [sb('p2_sq%d' % i, [128, 512], BF16) for i in range(2)]
        d2e = [sb('p2_d2e%d' % i, [128, 512], F32) for i in range(2)]
        vv = [sb('p2_vv%d' % i, [128, 512], F32) for i in range(2)]
        psS = [ps('p2_psS%d' % i, [128, 512], F32) for i in range(2)]
        psOA = [ps('p2_psOA%d' % i, [128, 512], F32) for i in range(2)]
        psOB = [ps('p2_psOB%d' % i, [128, 512], F32) for i in range(2)]
        psV = ps('p2_psV', [128, 8, 128], BF16)
        psF = ps('p2_psF', [128, 512], F32)

        P.dma('pool', ident[:], T_['ident'][:, :], 'ident', writes=['ident'], track=False)
        P.dma('pool', mask[:], T_['mask'][:, :], 'mask', writes=['mask'], track=False)
        P.dma('pool', ones_bd[:], T_['onesbd'][:, :], 'onesbd', writes=['onesbd'], track=False)
        P.dma('sp', gA[:], T_['gA'][:, :], 'gA', writes=['gA'], track=False)
        for i in range(NPT):
            P.memset('pool', VB[i][:, 64:192], 1.0, ['VBones%d' % i])

        P.memset('pool', KTA[64:128, :], 0.0, ['KTz'])
        P.memset('pool', KTB[0:64, :], 0.0, ['KTz'])
        cnt = 0
        for pr in range(npairs):
            P.dma('sp', QT[:], T_['QTd'][pr, :, :], 'QT', writes=['QT'])
            P.dma('sp', KTA[0:64, :], T_['KTd'][pr, 0:64, :], 'KTA', writes=['KT'])
            P.dma('sp', KTB[64:128, :], T_['KTd'][pr, 64:128, :], 'KTB', writes=['KT'])
            P.dma('sp', VT[:], T_['VTd'][pr, :, :], 'VT', writes=['VT'])
            og = 0
            for pi, r in enumerate((1, 4, 16)):
                if str(r) not in os.environ.get('P2_PAT', '1,4,16').split(','):
                    continue
                nb = 32 // r
                for c in range(r):
                    prev = None
                    for b in range(nb):
                        base = 128 * r * b + c
                        sl_k = slice(base, base + 127 * r + 1, r)
                        nq = 256 if b + 1 < nb else 128
                        sl_q = slice(base, base + (nq - 1) * r + 1, r)
                        bi = cnt % NPT
                        si = cnt % 2
                        cnt += 1
                        vslot = bi
                        P.tr(psV[:, vslot, :], VT[:, sl_k], ident[:], ['VT', 'ident'], ['psV'])
                        P.cp('act', VB[bi][:, 0:64], psV[:, vslot, 0:64], ['psV'], ['VB%d' % bi])
                        P.cp('act', VB[bi][:, 192:256], psV[:, vslot, 64:128], ['psV'], ['VB%d' % bi])
                        P.mm(psS[si][:, 0:nq], KTA[:, sl_k], QT[:, sl_q], True, True, ['KT', 'KTz', 'QT'], ['psS%d' % si])
                        P.mm(psS[si][:, 256:256 + nq], KTB[:, sl_k], QT[:, sl_q], True, True, ['KT', 'KTz', 'QT'], ['psS%d' % si])
                        if nq == 256:
                            P.act(PT[bi][:], psS[si][:], AF.Exp, ['psS%d' % si], ['PT%d' % bi], scale=0.125)
                            P.tt('dve', PT[bi][:], PT[bi][:], mask[:], ALU.mult, ['PT%d' % bi, 'mask'], ['PT%d' % bi])
                        else:
                            pv = PT[bi][:].rearrange("p (a n) -> p a n", a=2)[:, :, 0:128]
                            sv = psS[si][:].rearrange("p (a n) -> p a n", a=2)[:, :, 0:128]
                            mv = mask[:].rearrange("p (a n) -> p a n", a=2)[:, :, 0:128]
                            P.act(pv, sv, AF.Exp, ['psS%d' % si], ['PT%d' % bi], scale=0.125)
                            P.tt('dve', pv, pv, mv, ALU.mult, ['PT%d' % bi, 'mask'], ['PT%d' % bi])
                        g, gi = divmod(b, 4)
                        oi = (og + g) % 2
                        cs = slice(gi * 128, (gi + 1) * 128)
                        for hh, (pso, pk, vs, qoff) in enumerate(((psOA[oi], 'psOA%d' % oi, slice(0, 128), 0),
                                                                   (psOB[oi], 'psOB%d' % oi, slice(128, 256), 256))):
                            if prev is not None:
                                P.mm(pso[:, cs], VB[prev][:, vs], PT[prev][:, qoff + 128:qoff + 256], True, False,
                                     ['VB%d' % prev, 'VBones%d' % prev, 'PT%d' % prev], [pk])
                            P.mm(pso[:, cs], VB[bi][:, vs], PT[bi][:, qoff:qoff + 128], prev is None, True,
                                 ['VB%d' % bi, 'VBones%d' % bi, 'PT%d' % bi], [pk])
                        prev = bi
                        if gi == 3 or b == nb - 1:
                            ncols = (gi + 1) * 128
                            t0 = 128 * r * (4 * g) + c
                            asl = slice(t0, t0 + (ncols - 1) * r + 1, r)
                            for hh, (pso, pk) in enumerate(((psOA[oi], 'psOA%d' % oi), (psOB[oi], 'psOB%d' % oi))):
                                if pi == 0:
                                    P.cp('dve', ACC[hh][:, asl], pso[:, 0:ncols], [pk], ['ACC%d' % hh])
                                else:
                                    P.tt('dve', ACC[hh][:, asl], pso[:, 0:ncols], ACC[hh][:, asl], ALU.add, [pk, 'ACC%d' % hh], ['ACC%d' % hh])
                    og += (nb + 3) // 4
            if os.environ.get('P2_FIN', '1') == '0':
                continue
            P.dma('sp', DEN[0:64, :], ACC[0][64:128, :], 'den0', reads=['ACC0'], writes=['DEN'])
            P.dma('sp', DEN[64:128, :], ACC[1][0:64, :], 'den1', reads=['ACC1'], writes=['DEN'])
            for j in range(8):
                js = slice(j * 512, (j + 1) * 512)
                k = j % 2
                P.act(sq[k][0:64, :], ACC[0][0:64, js], AF.Square, ['ACC0'], ['sq%d' % k])
                P.act(sq[k][64:128, :], ACC[1][64:128, js], AF.Square, ['ACC1'], ['sq%d' % k])
                P.mm(psF[:], ones_bd[:], sq[k][:], True, True, ['onesbd', 'sq%d' % k], ['psF'])
                P.stt(d2e[k][:], DEN[:, js], RMS_EPS, DEN[:, js], ALU.mult, ALU.mult, ['DEN'], ['d2e%d' % k])
                P.stt(vv[k][:], psF[:], 1.0 / 64, d2e[k][:], ALU.mult, ALU.add, ['psF', 'd2e%d' % k], ['vv%d' % k])
                P.act(vv[k][:], vv[k][:], AF.Sqrt, ['vv%d' % k], ['vv%d' % k])
                P.recip(vv[k][:], vv[k][:], ['vv%d' % k], ['vv%d' % k])
                P.stt(yTp[0:64, js], ACC[0][0:64, js], gA[0:64, pr:pr + 1], vv[k][0:64, :], ALU.mult, ALU.mult,
                      ['ACC0', 'gA', 'vv%d' % k], ['yTp'])
                P.stt(yTp[64:128, js], ACC[1][64:128, js], gA[64:128, pr:pr + 1], vv[k][64:128, :], ALU.mult, ALU.mult,
                      ['ACC1', 'gA', 'vv%d' % k], ['yTp'])
            P.dma('sp', T_['YTd'][pr, :, :], yTp[:], 'oyT', reads=['yTp'])
        P.emit()


def post_norm_residual(P, psY, keys, gtab, xres, xres_keys, out_tile, out_key, st, tmp, junk, pfx):
    for hf in range(2):
        P.act(junk[:, hf * 512:(hf + 1) * 512], psY[hf][:], AF.Square, [keys[hf]], ['junk', pfx + 'ss%d' % hf], accum=st[:, hf:hf + 1])
    P.tt('dve', st[:, 2:3], st[:, 0:1], st[:, 1:2], ALU.add, [pfx + 'ss0', pfx + 'ss1'], [pfx + 'sst'])
    P.rstd(st[:, 2:3], st[:, 3:4], st[:, 4:5], 1.0 / D, RMS_EPS, pfx + 'sst')
    for hf in range(2):
        hs = slice(hf * 512, (hf + 1) * 512)
        P.stt(tmp[:, hs], psY[hf][:], st[:, 4:5], gtab[:, hs], ALU.mult, ALU.mult, [keys[hf], pfx + 'sst_r'], [pfx + 'tmp%d' % hf])
        P.tt('pool', out_tile[:, hs], tmp[:, hs], xres[:, hs], ALU.add, [pfx + 'tmp%d' % hf] + list(xres_keys), [out_key])


def pre_norm_T(P, xsrc, xkey, gtab, st, junk, hb, hbkey, psT, ident, hT, tsl, hTkey, pfx):
    P.act(junk[:], xsrc, AF.Square, [xkey], ['junk', pfx + 'ss'], accum=st[:, 8:9])
    P.rstd(st[:, 8:9], st[:, 9:10], st[:, 10:11], 1.0 / D, RMS_EPS, pfx + 'ss')
    P.stt(hb[:], xsrc, st[:, 10:11], gtab[:], ALU.mult, ALU.mult, [xkey, pfx + 'ss_r'], [hbkey])
    for dc in range(8):
        P.tr(psT[:, dc, :], hb[:, dc * 128:(dc + 1) * 128], ident[:], [hbkey, 'ident'], ['psT'])
    P.cp('act', hT[:, :, tsl], psT[:], ['psT'], [hTkey])


def pass0(P, nc, T_):
    with contextlib.ExitStack() as es:
        sb = lambda n, s, d: es.enter_context(nc.sbuf_tensor(n, s, d))
        ps = lambda n, s, d: es.enter_context(nc.psum_tensor(n, s, d))
        P.begin_pass()
        KmT, Vm = T_['KmT'], T_['Vm']
        ident = sb('p0_id', [128, 128], BF16)
        Wkv = sb('p0_wkv', [128, 8, 2048], BF16)
        gmn = sb('p0_gmn', [128, 1024], F32)
        mhT = sb('p0_mhT', [128, 8, 256], BF16)
        xt = [sb('p0_xt%d' % i, [128, 1024], F32) for i in range(2)]
        junk = sb('p0_junk', [128, 1024], BF16)
        st = sb('p0_st', [128, 16], F32)
        hb = sb('p0_hb', [128, 1024], BF16)
        psY = [ps('p0_psY%d' % i, [128, 512], F32) for i in range(2)]
        psQ = [ps('p0_psQ%d' % i, [128, 512], F32) for i in range(2)]
        psT = ps('p0_psT', [128, 8, 128], BF16)
        P.dma('pool', ident[:], T_['ident'][:, :], 'ident', writes=['ident'], track=False)
        load_w(P, Wkv, T_['w_kv'], 8, 'Wkv')
        P.dma('sp', gmn[:], T_['gt'][:, 4, :], 'g0', writes=['gmn'], track=False)
        for mt in range(2):
            P.dma('sp', xt[mt][:], T_['mem'][mt * 128:(mt + 1) * 128, :], 'xt%d' % mt, writes=['xt%d' % mt])
            pre_norm_T(P, xt[mt][:], 'xt%d' % mt, gmn, st, junk, hb, 'hb', psT, ident, mhT, slice(mt * 128, (mt + 1) * 128), 'mhT', 'm')
        for fc in range(8):
            k = fc % 2
            for dc in range(8):
                P.mm(psQ[k][:, 0:256], Wkv[:, dc, fc * 128:(fc + 1) * 128], mhT[:, dc, :], dc == 0, dc == 7, ['Wkv', 'mhT'], ['psQ%d' % k])
            P.cp('act', KmT[:, fc, :], psQ[k][:, 0:256], ['psQ%d' % k], ['KmT'])
        for mt in range(2):
            for hf in range(2):
                for dc in range(8):
                    P.mm(psY[hf][:], mhT[:, dc, mt * 128:(mt + 1) * 128], Wkv[:, dc, 1024 + hf * 512:1024 + (hf + 1) * 512], dc == 0, dc == 7,
                         ['Wkv', 'mhT'], ['psY%d' % hf])
                P.cp('act', Vm[:, mt, hf * 512:(hf + 1) * 512], psY[hf][:], ['psY%d' % hf], ['Vm'])

        P.emit()


def pass3(P, nc, T_, nblk=8):
    x = T_['x']
    with contextlib.ExitStack() as es:
        sb = lambda n, s, d: es.enter_context(nc.sbuf_tensor(n, s, d))
        ps = lambda n, s, d: es.enter_context(nc.psum_tensor(n, s, d))
        P.begin_pass()
        ident = sb('p3_id', [128, 128], BF16)
        ones = sb('p3_ones', [128, 128], BF16)
        Wout = sb('p3_wout', [128, 8, 1024], BF16)
        Wq = sb('p3_wq', [128, 8, 1024], BF16)
        Wo = sb('p3_wo', [128, 8, 1024], BF16)
        gpost = sb('p3_gpost', [128, 1024], F32)
        gpm = sb('p3_gpm', [128, 1024], F32)
        gpostm = sb('p3_gpostm', [128, 1024], F32)
        yTb = [sb('p3_yTb%d' % i, [128, 8, 512], BF16) for i in range(2)]
        xt = [sb('p3_xt%d' % i, [128, 1024], F32) for i in range(2)]
        x1 = sb('p3_x1', [128, 4, 1024], F32)
        x2 = [sb('p3_x2%d' % i, [128, 1024], F32) for i in range(2)]
        tmp = sb('p3_tmp', [128, 1024], F32)
        junk = sb('p3_junk', [128, 1024], BF16)
        st = sb('p3_st', [128, 16], F32)
        hb = sb('p3_hb', [128, 1024], BF16)
        h2T = sb('p3_h2T', [128, 8, 512], BF16)
        qT = sb('p3_qT', [128, 8, 512], BF16)
        PTm = sb('p3_PTm', [128, 8, 512], BF16)
        oT = sb('p3_oT', [128, 8, 512], BF16)
        rden = [sb('p3_rden%d' % i, [128, 512], F32) for i in range(2)]
        psY = [ps('p3_psY%d' % i, [128, 512], F32) for i in range(2)]
        psT = ps('p3_psT', [128, 8, 128], BF16)
        psQ = [ps('p3_psQ%d' % i, [128, 512], F32) for i in range(2)]
        psO = [ps('p3_psO%d' % i, [128, 512], F32) for i in range(2)]
        psD = ps('p3_psD', [128, 512], F32)

        P.dma('pool', ident[:], T_['ident'][:, :], 'ident', writes=['ident'], track=False)
        load_w(P, Wout, T_['w_out'], 8, 'Wout')
        load_w(P, Wq, T_['w_q'], 8, 'Wq')
        load_w(P, Wo, T_['w_o'], 8, 'Wo')
        P.memset('pool', ones[:], 1.0, ['ones'])
        KmT, Vm = T_['KmT'], T_['Vm']
        for k, (t, gi) in enumerate(((gpost, 1), (gpm, 2), (gpostm, 3))):
            P.dma('sp', t[:], T_['gt'][:, gi, :], 'g%d' % k, writes=['g%d' % gi], track=False)

        for blk in range(nblk):
            bi = blk % 2
            bs = slice(blk * 512, (blk + 1) * 512)
            P.dma('sp', yTb[bi][:], T_['YTd'][:, :, bs].rearrange("a p n -> p a n"), 'yTb%d' % bi, writes=['yTb%d' % bi])
            for t in range(4):
                T = 4 * blk + t
                i = T % 2
                tsl = slice(t * 128, (t + 1) * 128)
                P.dma('sp', xt[i][:], x[T * 128:(T + 1) * 128, :], 'xt%d' % i, writes=['xt%d' % i])
                for hf in range(2):
                    for fc in range(8):
                        P.mm(psY[hf][:], yTb[bi][:, fc, tsl], Wout[:, fc, hf * 512:(hf + 1) * 512], fc == 0, fc == 7,
                             ['yTb%d' % bi, 'Wout'], ['psY%d' % hf])
                post_norm_residual(P, psY, ['psY0', 'psY1'], gpost, xt[i], ['xt%d' % i], x1[:, t, :], 'x1_%d' % t, st, tmp, junk, 'a')
                pre_norm_T(P, x1[:, t, :], 'x1_%d' % t, gpm, st, junk, hb, 'hb', psT, ident, h2T, tsl, 'h2T', 'b')
            for fc in range(8):
                k = fc % 2
                for dc in range(8):
                    P.mm(psQ[k][:], Wq[:, dc, fc * 128:(fc + 1) * 128], h2T[:, dc, :], dc == 0, dc == 7, ['Wq', 'h2T'], ['psQ%d' % k])
                P.cp('act', qT[:, fc, :], psQ[k][:], ['psQ%d' % k], ['qT%d' % fc])
            for h in range(4):
                for mt in range(2):
                    k = (h * 2 + mt) % 2
                    for cc in range(2):
                        P.mm(psQ[k][:], KmT[:, 2 * h + cc, mt * 128:(mt + 1) * 128], qT[:, 2 * h + cc, :], cc == 0, cc == 1,
                             ['KmT', 'qT%d' % (2 * h + cc)], ['psQ%d' % k])
                    P.act(PTm[:, h * 2 + mt, :], psQ[k][:], AF.Exp, ['psQ%d' % k], ['PTm%d' % h], scale=1.0 / 16)
            for h in range(4):
                for mt in range(2):
                    P.mm(psD[:], ones[:], PTm[:, h * 2 + mt, :], mt == 0, mt == 1, ['ones', 'PTm%d' % h], ['psD'])
                P.recip(rden[h % 2][:], psD[:], ['psD'], ['rden%d' % (h % 2)])
                for cc in range(2):
                    fcx = 2 * h + cc
                    for mt in range(2):
                        P.mm(psO[cc][:], Vm[:, mt, fcx * 128:(fcx + 1) * 128], PTm[:, h * 2 + mt, :], mt == 0, mt == 1,
                             ['Vm', 'PTm%d' % h], ['psO%d' % cc])
                    P.tt('dve', oT[:, fcx, :], psO[cc][:], rden[h % 2][:], ALU.mult, ['psO%d' % cc, 'rden%d' % (h % 2)], ['oT'])
            for t in range(4):
                T = 4 * blk + t
                i = T % 2
                tsl = slice(t * 128, (t + 1) * 128)
                for hf in range(2):
                    for fc in range(8):
                        P.mm(psY[hf][:], oT[:, fc, tsl], Wo[:, fc, hf * 512:(hf + 1) * 512], fc == 0, fc == 7, ['oT', 'Wo'], ['psY%d' % hf])
                post_norm_residual(P, psY, ['psY0', 'psY1'], gpostm, x1[:, t, :], ['x1_%d' % t], x2[i], 'x2_%d' % i, st, tmp, junk, 'c')
                P.dma('sp', T_['X2d'][T * 128:(T + 1) * 128, :], x2[i][:], 'ox2_%d' % i, reads=['x2_%d' % i])
        P.emit()


def pass4(P, nc, T_, nblk=8):
    with contextlib.ExitStack() as es:
        sb = lambda n, s, d: es.enter_context(nc.sbuf_tensor(n, s, d))
        ps = lambda n, s, d: es.enter_context(nc.psum_tensor(n, s, d))
        P.begin_pass()
        ident = sb('p4_id', [128, 128], BF16)
        Wgu = sb('p4_wgu', [128, 8, 2 * FFN], BF16)
        Wd = sb('p4_wd', [128, NFC, 1024], BF16)
        gpf = sb('p4_gpf', [128, 1024], F32)
        gpostf = sb('p4_gpostf', [128, 1024], F32)
        xt = [sb('p4_xt%d' % i, [128, 1024], F32) for i in range(2)]
        ot = [sb('p4_ot%d' % i, [128, 1024], F32) for i in range(2)]
        tmp = sb('p4_tmp', [128, 1024], F32)
        junk = sb('p4_junk', [128, 1024], BF16)
        st = sb('p4_st', [128, 16], F32)
        hb = sb('p4_hb', [128, 1024], BF16)
        h3T = sb('p4_h3T', [128, 8, 512], BF16)
        actT = sb('p4_actT', [128, NFC, 512], BF16)
        sg = [sb('p4_sg%d' % i, [128, 512], F32) for i in range(2)]
        psY = [ps('p4_psY%d' % i, [128, 512], F32) for i in range(2)]
        psT = ps('p4_psT', [128, 8, 128], BF16)
        psG = [ps('p4_psG%d' % i, [128, 512], F32) for i in range(2)]
        psU = [ps('p4_psU%d' % i, [128, 512], F32) for i in range(2)]

        P.dma('pool', ident[:], T_['ident'][:, :], 'ident', writes=['ident'], track=False)
        load_w(P, Wgu, T_['w_gu'], 8, 'Wgu')
        load_w(P, Wd, T_['w_dn'], NFC, 'Wd')
        P.dma('sp', gpf[:], T_['gt'][:, 5, :], 'g0', writes=['gpf'], track=False)
        P.dma('sp', gpostf[:], T_['gt'][:, 6, :], 'g1', writes=['gpostf'], track=False)

        for blk in range(nblk):
            for t in range(4):
                T = 4 * blk + t
                i = T % 2
                P.dma('sp', xt[i][:], T_['X2d'][T * 128:(T + 1) * 128, :], 'xt%d' % i, writes=['xt%d' % i])
                pre_norm_T(P, xt[i][:], 'xt%d' % i, gpf, st, junk, hb, 'hb', psT, ident, h3T, slice(t * 128, (t + 1) * 128), 'h3T', 'b')
            for j in range(NFC):
                k = j % 2
                for dc in range(8):
                    P.mm(psG[k][:], Wgu[:, dc, j * 128:(j + 1) * 128], h3T[:, dc, :], dc == 0, dc == 7, ['Wgu', 'h3T'], ['psG%d' % k])
                for dc in range(8):
                    P.mm(psU[k][:], Wgu[:, dc, FFN + j * 128:FFN + (j + 1) * 128], h3T[:, dc, :], dc == 0, dc == 7, ['Wgu', 'h3T'], ['psU%d' % k])
                P.act(sg[k][:], psG[k][:], AF.Silu, ['psG%d' % k], ['sg%d' % k])
                P.tt('dve', actT[:, j, :], psU[k][:], sg[k][:], ALU.mult, ['psU%d' % k, 'sg%d' % k], ['actT'])
            for t in range(4):
                T = 4 * blk + t
                i = T % 2
                tsl = slice(t * 128, (t + 1) * 128)
                P.dma('sp', xt[i][:], T_['X2d'][T * 128:(T + 1) * 128, :], 'xt%d' % i, writes=['xt%d' % i])
                for hf in range(2):
                    for fc in range(NFC):
                        P.mm(psY[hf][:], actT[:, fc, tsl], Wd[:, fc, hf * 512:(hf + 1) * 512], fc == 0, fc == NFC - 1, ['actT', 'Wd'], ['psY%d' % hf])
                post_norm_residual(P, psY, ['psY0', 'psY1'], gpostf, xt[i], ['xt%d' % i], ot[i], 'ot%d' % i, st, tmp, junk, 'c')
                P.dma('sp', T_['out'][T * 128:(T + 1) * 128, :], ot[i][:], 'oo%d' % i, reads=['ot%d' % i])
        P.emit()


def host_tables():
    pos = np.arange(S, dtype=np.float32)
    invA = (np.float32(500000.0) ** (-(np.arange(0, 16, 2, dtype=np.float32) / np.float32(16)))).astype(np.float32)
    angA = (pos[:, None] * invA[None, :]).astype(np.float32).astype(np.float64)
    invR = (np.float32(10000.0) ** (-(np.arange(0, 128, 2, dtype=np.float32) / np.float32(128)))).astype(np.float32)
    angR = (pos[:, None] * invR[None, :]).astype(np.float32).astype(np.float64)

    def lay(a):
        return np.ascontiguousarray(a.reshape(NT, 128, -1).transpose(1, 0, 2)).astype(np.float32)
    tabs = {'cosA': lay(np.cos(angA)), 'sinA': lay(np.sin(angA)), 'cosR': lay(np.cos(angR)), 'sinR': lay(np.sin(angR))}
    H = 4
    log_g = np.log(1.0 - 2.0 ** (-5.0 - np.arange(H, dtype=np.float64)))
    n = np.arange(128, dtype=np.float64)
    sc = 128.0 ** -0.5
    decT = np.zeros((128, H, 128), np.float64)
    for h in range(H):
        rel = n[None, :] - n[:, None]
        decT[:, h, :] = np.where(rel >= 0, np.exp(log_g[h] * np.maximum(rel, 0.0)), 0.0) * sc
    xi = np.exp(log_g[:, None] * (n[None, :] + 1.0))
    zeta = np.exp(log_g[:, None] * (127.0 - n[None, :])) * sc
    tabs['decT'] = decT.astype(np.float32)
    tabs['xiT'] = np.ascontiguousarray(np.broadcast_to(xi[None, :, :], (128, H, 128))).astype(np.float32)
    tabs['zet'] = np.ascontiguousarray(np.broadcast_to(zeta.T[:, :, None], (128, H, 128))).astype(np.float32)
    tabs['cd'] = np.exp(log_g * 128.0)
    tabs['ident'] = np.eye(128, dtype=np.float32)
    j = np.arange(128)[:, None]
    i = np.arange(128)[None, :]
    cur = (j <= i).astype(np.float32)
    prv = (j >= i).astype(np.float32)
    tabs['mask'] = np.concatenate([cur, prv, cur, prv], axis=1)
    bd = np.zeros((128, 128), np.float32)
    bd[0:64, 0:64] = 1.0
    bd[64:128, 64:128] = 1.0
    tabs['onesbd'] = bd
    return tabs


def build(stages=(0, 1, 2, 3, 4), dbg=False, tabs=None):
    nc = bass.Bass("TRN2", target_bir_lowering=False)
    T_ = {}

    def inp(name, shape):
        T_[name] = nc.dram_tensor(name, list(shape), F32, kind="ExternalInput").ap()
    inp('x', [S, D])
    inp('mem', [256, D])
    inp('w_in', [D, 3584])
    inp('w_out', [D, D])
    inp('w_q', [D, D])
    inp('w_kv', [D, 2 * D])
    inp('w_o', [D, D])
    inp('w_gu', [D, 2 * FFN])
    inp('w_dn', [FFN, D])
    inp('gt', [128, 10, D])
    inp('gA', [128, 4])
    for nm in ('cosA', 'sinA'):
        inp(nm, [128, 32, 8])
    for nm in ('cosR', 'sinR'):
        inp(nm, [128, 32, 64])
    for nm in ('decT', 'xiT', 'zet'):
        inp(nm, [128, 4, 128])
    inp('ident', [128, 128])
    inp('mask', [128, 512])
    inp('onesbd', [128, 128])
    T_['cd'] = tabs['cd']
    T_['out'] = nc.dram_tensor('out', [S, D], F32, kind="ExternalOutput").ap()
    kind = "ExternalOutput" if dbg else "Internal"
    T_['QTd'] = nc.dram_tensor('QTd', [4, 128, S], BF16, kind=kind).ap()
    T_['KTd'] = nc.dram_tensor('KTd', [4, 128, S], BF16, kind=kind).ap()
    T_['VTd'] = nc.dram_tensor('VTd', [4, 128, S], BF16, kind=kind).ap()
    T_['YTd'] = nc.dram_tensor('YTd', [8, 128, S], BF16, kind=kind).ap()
    T_['X2d'] = nc.dram_tensor('X2d', [S, D], F32, kind=kind).ap()
    with contextlib.ExitStack() as es:
        P = Prog(nc, es)
        T_['KmT'] = es.enter_context(nc.sbuf_tensor('KmT', [128, 8, 256], BF16))
        T_['Vm'] = es.enter_context(nc.sbuf_tensor('Vm', [128, 2, 1024], BF16))
        if 0 in stages:
            pass0(P, nc, T_)
        if 1 in stages:
            pass1(P, nc, T_)
        if 2 in stages:
            pass2(P, nc, T_)
        if 3 in stages:
            pass3(P, nc, T_)
        if 4 in stages:
            pass4(P, nc, T_)
    return nc


def make_inputs(x, mem, pre_mix_g, post_mix_g, w_in, attn_gn_g, ret_gn_g, w_out,
                pre_mem_g, post_mem_g, mem_norm_g, w_q_mem, w_kv_mem, w_o_mem,
                pre_ffn_g, post_ffn_g, w_gate_up, w_down, tabs):
    f = lambda a: np.ascontiguousarray(np.asarray(a, dtype=np.float32))
    gt = np.zeros((128, 10, D), np.float32)
    for k, g in enumerate((pre_mix_g, post_mix_g, pre_mem_g, post_mem_g, mem_norm_g, pre_ffn_g, post_ffn_g)):
        gt[:, k, :] = np.broadcast_to(f(g)[0][None, :], (128, D))
    gt[:, 9, 0:512] = np.broadcast_to(f(ret_gn_g)[0][None, :], (128, 512))
    gA = np.ascontiguousarray(f(attn_gn_g)[0].reshape(4, 128).T)
    shared = {
        'w_in': f(w_in)[0], 'w_out': f(w_out)[0], 'w_q': f(w_q_mem)[0], 'w_kv': f(w_kv_mem)[0], 'w_o': f(w_o_mem)[0],
        'w_gu': f(w_gate_up)[0], 'w_dn': f(w_down)[0], 'gt': gt, 'gA': gA,
    }
    for nm in ('cosA', 'sinA', 'cosR', 'sinR', 'decT', 'xiT', 'zet', 'ident', 'mask', 'onesbd'):
        shared[nm] = tabs[nm]
    xs = f(x)
    ms = f(mem)
    return [dict(shared, x=xs[b], mem=ms[b]) for b in range(xs.shape[0])]


def kernel(**inputs):
    tabs = host_tables()
    in_maps = make_inputs(tabs=tabs, **inputs)
    nc = build(tabs=tabs)
    res = run_bass_kernel_spmd(nc, in_maps, core_ids=list(range(8)))
    return np.stack([np.asarray(r['out'], dtype=np.float32) for r in res.results], axis=0)
```

```python
import contextlib
import os
import numpy as np
import concourse.bass as bass
import concourse.mybir as mybir
from concourse.bass_utils import run_bass_kernel_spmd

F32 = mybir.dt.float32
BF16 = mybir.dt.bfloat16
AF = mybir.ActivationFunctionType
ALU = mybir.AluOpType

S = 4096
D = 1024
NT = 32
RMS_EPS = 1e-6
GN_EPS = 1e-5
FFN = 2816
NFC = 22


class Prog:
    ENG = ('pe', 'act', 'dve', 'pool', 'sp')
    LIM = 16000

    def __init__(self, nc, es):
        self.nc = nc
        self.es = es
        self.esem = {e: [] for e in ('pe', 'act', 'dve', 'pool')}
        self.dsem = {}
        self.sigcount = {e: 0 for e in ('pe', 'act', 'dve', 'pool')}
        self.dma_cnt = {}
        self.begin_pass()

    def begin_pass(self):
        self.q = {e: [] for e in self.ENG}
        self.last_w = {}
        self.readers = {}
        self.waited = {e: {} for e in self.ENG}
        self.pass_dma = {}

    def _dep(self, ins, sig):
        if sig is None:
            return
        qn = ins['q']
        kind, ref, val = sig
        if kind == 'eng' and ref == 'pe' and qn == 'pe':
            return
        if self.waited[qn].get((kind, ref), -1) >= val:
            return
        self.waited[qn][(kind, ref)] = val
        ins['waits'].append(sig)
        if kind == 'eng':
            self.q[ref][val]['signal'] = True

    def add(self, qn, fn, reads=(), writes=(), dma_slot=None, track=True):
        ins = {'q': qn, 'fn': fn, 'waits': [], 'signal': False, 'dma_slot': dma_slot}
        if dma_slot is None:
            sig = ('eng', qn, len(self.q[qn]))
        else:
            c = self.dma_cnt.get(dma_slot, 0) + 1
            assert c < 2000, dma_slot
            self.dma_cnt[dma_slot] = c
            sig = ('dma', dma_slot, 16 * c)
            self.pass_dma[dma_slot] = 16 * c
        ins['sig'] = sig
        if track:
            for k in reads:
                self._dep(ins, self.last_w.get(k))
                if k.startswith('ps'):
                    for r in self.readers.get(k, ()):
                        if not (r[0] == 'eng' and r[1] == qn):
                            self._dep(ins, r)
            for k in writes:
                self._dep(ins, self.last_w.get(k))
                for r in self.readers.get(k, ()):
                    self._dep(ins, r)
        for k in reads:
            self.readers.setdefault(k, []).append(sig)
        for k in writes:
            self.last_w[k] = sig
            self.readers[k] = []
        self.q[qn].append(ins)
        return sig

    def _esem(self, e, idx):
        while len(self.esem[e]) <= idx:
            self.esem[e].append(self.es.enter_context(self.nc.semaphore('s_%s%d' % (e, len(self.esem[e])))))
        return self.esem[e][idx]

    def _dsem(self, slot):
        if slot not in self.dsem:
            self.dsem[slot] = self.es.enter_context(self.nc.semaphore('d_%d' % len(self.dsem)))
        return self.dsem[slot]

    def emit(self):
        nc = self.nc
        for e in self.esem:
            c = self.sigcount[e]
            for ins in self.q[e]:
                if ins['dma_slot'] is None and ins['signal']:
                    ins['semidx'] = c // self.LIM
                    ins['sigval'] = c % self.LIM + 1
                    c += 1
            self.sigcount[e] = c

        def resolve(sig):
            kind, ref, val = sig
            if kind == 'eng':
                i = self.q[ref][val]
                return self._esem(ref, i['semidx']), i['sigval']
            return self._dsem(ref), val

        final = [('dma', s, v) for s, v in self.pass_dma.items()]
        for e in self.ENG:
            for ins in self.q[e]:
                for w in ins['waits']:
                    resolve(w)
                if ins['dma_slot'] is not None:
                    self._dsem(ins['dma_slot'])
                elif ins['signal']:
                    self._esem(e, ins['semidx'])

        def run(e, eng):
            for ins in self.q[e]:
                for w in ins['waits']:
                    s, v = resolve(w)
                    eng.wait_ge(s, v)
                bi = ins['fn'](eng)
                if ins['dma_slot'] is not None:
                    bi.then_inc(self._dsem(ins['dma_slot']), 16)
                elif ins['signal']:
                    bi.then_inc(self._esem(e, ins['semidx']), 1)
            if e == 'sp':
                for sg in final:
                    s, v = resolve(sg)
                    eng.wait_ge(s, v)

        with nc.Block() as block:
            @block.tensor
            def _(eng):
                run('pe', eng)

            @block.scalar
            def _(eng):
                run('act', eng)

            @block.vector
            def _(eng):
                run('dve', eng)

            @block.gpsimd
            def _(eng):
                run('pool', eng)

            @block.sync
            def _(eng):
                run('sp', eng)

    def mm(self, out, lhsT, rhs, start, stop, reads, writes):
        return self.add('pe', lambda e: e.matmul(out, lhsT=lhsT, rhs=rhs, start=start, stop=stop), reads, writes)

    def tr(self, out, in_, ident, reads, writes):
        return self.add('pe', lambda e: e.transpose(out=out, in_=in_, identity=ident), reads, writes)

    def act(self, out, in_, func, reads, writes, scale=None, accum=None):
        def f(e):
            kw = {}
            if scale is not None:
                kw['scale'] = scale
            if accum is not None:
                kw['accum_out'] = accum
            return e.activation(out=out, in_=in_, func=func, **kw)
        return self.add('act', f, reads, writes)

    def tt(self, q, out, in0, in1, op, reads, writes):
        return self.add(q, lambda e: e.tensor_tensor(out=out, in0=in0, in1=in1, op=op), reads, writes)

    def ts(self, q, out, in0, s1, s2, op0, op1, reads, writes):
        if s2 is None:
            return self.add(q, lambda e: e.tensor_scalar(out=out, in0=in0, scalar1=s1, scalar2=None, op0=op0), reads, writes)
        return self.add(q, lambda e: e.tensor_scalar(out=out, in0=in0, scalar1=s1, scalar2=s2, op0=op0, op1=op1), reads, writes)

    def stt(self, out, in0, scalar, in1, op0, op1, reads, writes):
        return self.add('dve', lambda e: e.scalar_tensor_tensor(out=out, in0=in0, scalar=scalar, in1=in1, op0=op0, op1=op1), reads, writes)

    def cp(self, q, out, in_, reads, writes):
        if q == 'act':
            return self.add('act', lambda e: e.copy(out=out, in_=in_), reads, writes)
        return self.add(q, lambda e: e.tensor_copy(out=out, in_=in_), reads, writes)

    def recip(self, out, in_, reads, writes):
        return self.add('dve', lambda e: e.reciprocal(out=out, in_=in_), reads, writes)

    def dma(self, q, out, in_, slot, reads=(), writes=(), track=True):
        return self.add(q, lambda e: e.dma_start(out=out, in_=in_), reads, writes, dma_slot=slot, track=track)

    def memset(self, q, ap, val, writes):
        return self.add(q, lambda e: e.memset(ap, val), (), writes)

    def rstd(self, ss_ap, tmp_ap, out_ap, mult, eps, key):
        self.ts('dve', tmp_ap, ss_ap, mult, eps, ALU.mult, ALU.add, [key], [key + '_t'])
        self.act(tmp_ap, tmp_ap, AF.Sqrt, [key + '_t'], [key + '_t'])
        self.recip(out_ap, tmp_ap, [key + '_t'], [key + '_r'])


def load_w(P, dst, src, nchunk, key):
    for dc in range(nchunk):
        P.dma('pool', dst[:, dc, :], src[dc * 128:(dc + 1) * 128, :], slot=key, writes=[key], track=False)


def load_w_cols(P, dst, src, key, groups):
    for g in groups:
        cs = slice(g * 512, (g + 1) * 512)
        P.dma('pool', dst[:, :, cs], src[:, cs].rearrange("(dc p) n -> p dc n", p=128), slot='%s%d' % (key, g),
              writes=['%s%d' % (key, g)], track=False)


def pass1(P, nc, T_, dbg_tiles=NT):
    x, w_in = T_['x'], T_['w_in']
    with contextlib.ExitStack() as es:
        sb = lambda n, s, d: es.enter_context(nc.sbuf_tensor(n, s, d))
        ps = lambda n, s, d: es.enter_context(nc.psum_tensor(n, s, d))
        P.begin_pass()
        W = sb('p1_w', [128, 8, 3584], BF16)
        ident = sb('p1_id', [128, 128], BF16)
        cosR = sb('p1_cosR', [128, 32, 64], F32)
        sinR = sb('p1_sinR', [128, 32, 64], F32)
        cosA = sb('p1_cosA', [128, 32, 8], F32)
        sinA = sb('p1_sinA', [128, 32, 8], F32)
        decT = sb('p1_decT', [128, 4, 128], F32)
        xiT = sb('p1_xiT', [128, 4, 128], F32)
        zet = sb('p1_zet', [128, 4, 128], F32)
        gpre = sb('p1_gpre', [128, 1024], F32)
        gret = sb('p1_gret', [128, 512], F32)
        xt = [sb('p1_xt%d' % i, [128, 1024], F32) for i in range(2)]
        junk = sb('p1_junk', [128, 1024], BF16)
        st = sb('p1_st', [128, 16], F32)
        hb = [sb('p1_hb%d' % i, [128, 1024], BF16) for i in range(2)]
        hTb = [sb('p1_hT%d' % i, [128, 8, 512], BF16) for i in range(2)]
        tA = sb('p1_tA', [128, 4, 64], F32)
        qka = [sb('p1_qka%d' % i, [128, 2, 512], BF16) for i in range(2)]
        aqkT = [sb('p1_aqkT%d' % i, [128, 2, 4, 512], BF16) for i in range(2)]
        avT = [sb('p1_avT%d' % i, [128, 4, 512], BF16) for i in range(2)]
        yTr = [sb('p1_yTr%d' % i, [128, 4, 512], BF16) for i in range(2)]
        tR = sb('p1_tR', [128, 4, 256], F32)
        qkr = [sb('p1_qkr%d' % i, [128, 2, 512], BF16) for i in range(2)]
        kz = [sb('p1_kz%d' % i, [128, 4, 128], BF16) for i in range(2)]
        QKT = [sb('p1_QKT%d' % i, [128, 8, 128], BF16) for i in range(2)]
        QTxi = [sb('p1_QTxi%d' % i, [128, 4, 128], BF16) for i in range(2)]
        vb = [sb('p1_vb%d' % i, [128, 512], BF16) for i in range(2)]
        gs = sb('p1_gs', [128, 512], F32)
        gs2 = [sb('p1_gs2%d' % i, [128, 512], F32) for i in range(2)]
        innT = sb('p1_innT', [128, 4, 128], BF16)
        R = sb('p1_R', [128, 4, 128], F32)
        Rbf = sb('p1_Rbf', [128, 4, 128], BF16)
        osb = sb('p1_osb', [128, 512], F32)
        rn = sb('p1_rn', [128, 512], F32)
        rb = [sb('p1_rb%d' % i, [128, 512], BF16) for i in range(2)]
        gst = sb('p1_gst', [128, 32], F32)
        psQa = ps('p1_psQa', [128, 512], F32)
        psKa = ps('p1_psKa', [128, 512], F32)
        psQr = ps('p1_psQr', [128, 512], F32)
        psKr = ps('p1_psKr', [128, 512], F32)
        psVG = ps('p1_psVG', [128, 512], F32)
        psT = ps('p1_psT', [128, 8, 128], BF16)
        psS = ps('p1_psS', [128, 512], F32)
        psO = ps('p1_psO', [128, 512], F32)

        load_w_cols(P, W, w_in, 'W', [0, 1, 3, 4, 5, 6, 2])
        P.dma('pool', ident[:], T_['ident'][:, :], 'ident', writes=['ident'], track=False)
        for nm, t in (('cosR', cosR), ('sinR', sinR), ('cosA', cosA), ('sinA', sinA), ('decT', decT), ('xiT', xiT), ('zet', zet)):
            P.dma('sp', t[:], T_[nm][:, :, :], nm, writes=[nm], track=False)
        P.dma('sp', gpre[:], T_['gt'][:, 0, :], 'gpre', writes=['gpre'], track=False)
        P.dma('sp', gret[:], T_['gt'][:, 9, 0:512], 'gret', writes=['gret'], track=False)
        P.memset('pool', R[:], 0.0, ['R'])

        cd = T_['cd']

        def xload(n):
            i = n % 2
            P.dma('sp', xt[i][:], x[n * 128:(n + 1) * 128, :], 'xt%d' % i, writes=['xt%d' % i])

        def s0_chain(n):
            i = n % 2
            P.act(junk[:], xt[i][:], AF.Square, ['xt%d' % i], ['junk', 'ss'], accum=st[:, 0:1])
            P.rstd(st[:, 0:1], st[:, 1:2], st[:, 2:3], 1.0 / D, RMS_EPS, 'ss')
            P.stt(hb[i][:], xt[i][:], st[:, 2:3], gpre[:], ALU.mult, ALU.mult, ['xt%d' % i, 'ss_r', 'gpre'], ['hb%d' % i])

        def s0_pe(n):
            i = n % 2
            bi, t = (n // 4) % 2, n % 4
            for dc in range(8):
                P.tr(psT[:, dc, :], hb[i][:, dc * 128:(dc + 1) * 128], ident[:], ['hb%d' % i, 'ident'], ['psT'])
            P.cp('act', hTb[bi][:, :, t * 128:(t + 1) * 128], psT[:], ['psT'], ['hT%d_%d' % (bi, t)])

        def proj(n, psum, c0, key):
            bi, t = (n // 4) % 2, n % 4
            for dc in range(8):
                P.mm(psum[:], hTb[bi][:, dc, t * 128:(t + 1) * 128], W[:, dc, c0:c0 + 512], dc == 0, dc == 7,
                     ['hT%d_%d' % (bi, t), 'W%d' % (c0 // 512)], [key])

        def s1_att(n):
            i = n % 2
            proj(n, psQa, 0, 'psQa')
            proj(n, psKa, 512, 'psKa')
            cb = cosA[:, n, :].unsqueeze(1).broadcast_to([128, 8, 8])
            snb = sinA[:, n, :].unsqueeze(1).broadcast_to([128, 8, 8])
            for j, (pp, pk) in enumerate(((psQa, 'psQa'), (psKa, 'psKa'))):
                v = pp[:].rearrange("p (h d) -> p h d", h=8)
                x1, x2 = v[:, :, 0:8], v[:, :, 8:16]
                tv = [tA[:, k, :].rearrange("p (h d) -> p h d", h=8) for k in range(4)]
                P.tt('dve', tv[0], x1, cb, ALU.mult, [pk, 'cosA'], ['tA0'])
                P.tt('dve', tv[1], x2, snb, ALU.mult, [pk, 'sinA'], ['tA1'])
                P.tt('dve', tv[2], x2, cb, ALU.mult, [pk, 'cosA'], ['tA2'])
                P.tt('dve', tv[3], x1, snb, ALU.mult, [pk, 'sinA'], ['tA3'])
                o = qka[i][:, j, :].rearrange("p (h d) -> p h d", h=8)
                P.tt('pool', o[:, :, 0:8], tv[0], tv[1], ALU.subtract, ['tA0', 'tA1'], ['qka%d' % i])
                P.tt('pool', o[:, :, 8:16], tv[2], tv[3], ALU.add, ['tA2', 'tA3'], ['qka%d' % i])
                P.cp('act', o[:, :, 16:64], v[:, :, 16:64], [pk], ['qka%d' % i])

        def s2_att_tr(n):
            i = n % 2
            bi, t = (n // 4) % 2, n % 4
            for j in range(2):
                for pr in range(4):
                    P.tr(psT[:, j * 4 + pr, :], qka[i][:, j, pr * 128:(pr + 1) * 128], ident[:], ['qka%d' % i, 'ident'], ['psT'])
            P.cp('act', aqkT[bi][:, :, :, t * 128:(t + 1) * 128], psT[:].rearrange("p (a b) c -> p a b c", a=2), ['psT'], ['aqkT%d' % bi])

        def s1_ret(n):
            i = n % 2
            proj(n, psQr, 1536, 'psQr')
            proj(n, psKr, 2048, 'psKr')
            cb = cosR[:, n, :].unsqueeze(1).broadcast_to([128, 4, 64])
            snb = sinR[:, n, :].unsqueeze(1).broadcast_to([128, 4, 64])
            for j, (pp, pk) in enumerate(((psQr, 'psQr'), (psKr, 'psKr'))):
                v = pp[:].rearrange("p (h d) -> p h d", h=4)
                x1, x2 = v[:, :, 0:64], v[:, :, 64:128]
                tv = [tR[:, k, :].rearrange("p (h d) -> p h d", h=4) for k in range(4)]
                P.tt('dve', tv[0], x1, cb, ALU.mult, [pk, 'cosR'], ['tR0'])
                P.tt('dve', tv[1], x2, snb, ALU.mult, [pk, 'sinR'], ['tR1'])
                P.tt('dve', tv[2], x2, cb, ALU.mult, [pk, 'cosR'], ['tR2'])
                P.tt('dve', tv[3], x1, snb, ALU.mult, [pk, 'sinR'], ['tR3'])
                o = qkr[i][:, j, :].rearrange("p (h d) -> p h d", h=4)
                P.tt('pool', o[:, :, 0:64], tv[0], tv[1], ALU.subtract, ['tR0', 'tR1'], ['qkr%d' % i])
                P.tt('pool', o[:, :, 64:128], tv[2], tv[3], ALU.add, ['tR2', 'tR3'], ['qkr%d' % i])
            P.tt('pool', kz[i][:], qkr[i][:, 1, :].rearrange("p (h d) -> p h d", h=4), zet[:], ALU.mult, ['qkr%d' % i, 'zet'], ['kz%d' % i])

        def s2_ret_tr(n):
            i = n % 2
            for j in range(2):
                for h in range(4):
                    P.tr(psT[:, j * 4 + h, :], qkr[i][:, j, h * 128:(h + 1) * 128], ident[:], ['qkr%d' % i, 'ident'], ['psT'])
            P.cp('act', QKT[i][:], psT[:], ['psT'], ['QKT%d' % i])
            P.tt('dve', QTxi[i][:], psT[:, 0:4, :], xiT[:], ALU.mult, ['psT', 'xiT'], ['QTxi%d' % i])

        def s1_v(n):
            i = n % 2
            proj(n, psVG, 2560, 'psVG')
            P.cp('act', vb[i][:], psVG[:], ['psVG'], ['vb%d' % i])

        def s1_g(n):
            i = n % 2
            proj(n, psVG, 3072, 'psVG')
            P.act(gs[:], psVG[:], AF.Silu, ['psVG'], ['gs'])
            P.tt('pool', gs2[i][:], gs[:], gret[:], ALU.mult, ['gs', 'gret'], ['gs2%d' % i])

        def s2_scores(n):
            i = n % 2
            for h in range(4):
                hs = slice(h * 128, (h + 1) * 128)
                P.mm(psS[:, hs], QKT[i][:, 4 + h, :], QKT[i][:, h, :], True, True, ['QKT%d' % i], ['psS'])
            P.tt('dve', innT[:], psS[:].rearrange("p (h n) -> p h n", h=4), decT[:], ALU.mult, ['psS', 'decT'], ['innT'])

        def s2_out(n):
            i = n % 2
            for h in range(4):
                hs = slice(h * 128, (h + 1) * 128)
                P.mm(psO[:, hs], innT[:, h, :], vb[i][:, hs], True, n == 0, ['innT', 'vb%d' % i], ['psO'])
                if n > 0:
                    P.mm(psO[:, hs], QTxi[i][:, h, :], Rbf[:, h, :], False, True, ['QTxi%d' % i, 'Rbf'], ['psO'])
            if n < NT - 1:
                for h in range(4):
                    hs = slice(h * 128, (h + 1) * 128)
                    P.mm(psS[:, hs], kz[i][:, h, :], vb[i][:, hs], True, True, ['kz%d' % i, 'vb%d' % i], ['psS'])
                for h in range(4):
                    hs = slice(h * 128, (h + 1) * 128)
                    P.stt(R[:, h, :], R[:, h, :], float(cd[h]), psS[:, hs], ALU.mult, ALU.add, ['R', 'psS'], ['R'])
                P.cp('act', Rbf[:], R[:], ['R'], ['Rbf'])
            for h in range(4):
                hs = slice(h * 128, (h + 1) * 128)
                P.act(osb[:, hs], psO[:, hs], AF.Copy, ['psO'], ['osb', 'gst_s'], accum=gst[:, h:h + 1])
            for h in range(4):
                hs = slice(h * 128, (h + 1) * 128)
                P.act(junk[:, hs], psO[:, hs], AF.Square, ['psO'], ['junk', 'gst_q'], accum=gst[:, 4 + h:5 + h])
            P.ts('dve', gst[:, 8:12], gst[:, 0:4], 1.0 / 128, None, ALU.mult, None, ['gst_s'], ['gst_m'])
            P.tt('dve', gst[:, 12:16], gst[:, 8:12], gst[:, 8:12], ALU.mult, ['gst_m'], ['gst_m2'])
            P.stt(gst[:, 16:20], gst[:, 4:8], 1.0 / 128, gst[:, 12:16], ALU.mult, ALU.subtract, ['gst_q', 'gst_m2'], ['gst_v'])
            P.ts('dve', gst[:, 20:24], gst[:, 16:20], GN_EPS, None, ALU.add, None, ['gst_v'], ['gst_ve'])
            P.act(gst[:, 20:24], gst[:, 20:24], AF.Sqrt, ['gst_ve'], ['gst_ve'])
            P.recip(gst[:, 24:28], gst[:, 20:24], ['gst_ve'], ['gst_r'])
            for h in range(4):
                hs = slice(h * 128, (h + 1) * 128)
                P.ts('dve', rn[:, hs], osb[:, hs], gst[:, 8 + h:9 + h], gst[:, 24 + h:25 + h], ALU.subtract, ALU.mult,
                     ['osb', 'gst_m', 'gst_r'], ['rn'])
            P.tt('pool', rb[i][:], rn[:], gs2[i][:], ALU.mult, ['rn', 'gs2%d' % i], ['rb%d' % i])

        def s3_r_tr(n):
            i = n % 2
            bi, t = (n // 4) % 2, n % 4
            for h in range(4):
                P.tr(psT[:, h, :], rb[i][:, h * 128:(h + 1) * 128], ident[:], ['rb%d' % i, 'ident'], ['psT'])
            P.cp('act', yTr[bi][:, :, t * 128:(t + 1) * 128], psT[:, 0:4, :], ['psT'], ['yTr%d' % bi])

        def av_block(blk):
            bi = blk % 2
            hks = ['hT%d_%d' % (bi, t) for t in range(4)]
            for pr in range(4):
                for dc in range(8):
                    P.mm(psVG[:], W[:, dc, 1024 + pr * 128:1024 + (pr + 1) * 128], hTb[bi][:, dc, :], dc == 0, dc == 7, hks + ['W2'], ['psVG'])
                P.cp('act', avT[bi][:, pr, :], psVG[:], ['psVG'], ['avT%d' % bi])
            bs = slice(blk * 512, (blk + 1) * 512)
            P.dma('sp', T_['VTd'][:, :, bs].rearrange("a p n -> p a n"), avT[bi][:], 'ov%d' % bi, reads=['avT%d' % bi])

        def ok(n):
            return 0 <= n < NT

        xload(0)
        xload(1)
        s0_chain(0)
        s0_pe(0)
        for n in range(0, NT + 2):
            if ok(n + 2):
                xload(n + 2)
            if ok(n + 1):
                s0_chain(n + 1)
            if ok(n):
                s1_att(n)
            if ok(n - 1):
                s2_att_tr(n - 1)
                if (n - 1) % 4 == 3:
                    blk = (n - 1) // 4
                    bs = slice(blk * 512, (blk + 1) * 512)
                    P.dma('sp', T_['QTd'][:, :, bs].rearrange("a p n -> p a n"), aqkT[blk % 2][:, 0, :, :], 'oq%d' % (blk % 2), reads=['aqkT%d' % (blk % 2)])
                    P.dma('sp', T_['KTd'][:, :, bs].rearrange("a p n -> p a n"), aqkT[blk % 2][:, 1, :, :], 'ok%d' % (blk % 2), reads=['aqkT%d' % (blk % 2)])
            if ok(n):
                s1_ret(n)
            if ok(n - 1):
                s2_ret_tr(n - 1)
            if ok(n):
                s1_v(n)
            if ok(n - 1):
                s2_scores(n - 1)
            if ok(n):
                s1_g(n)
            if ok(n - 1):
                s2_out(n - 1)
            if ok(n - 2):
                s3_r_tr(n - 2)
                if (n - 2) % 4 == 3:
                    blk = (n - 2) // 4
                    bs = slice(blk * 512, (blk + 1) * 512)
                    P.dma('sp', T_['YTd'][4:8, :, bs].rearrange("a p n -> p a n"), yTr[blk % 2][:], 'oy%d' % (blk % 2), reads=['yTr%d' % (blk % 2)])
            if ok(n + 1):
                s0_pe(n + 1)
            if ok(n) and n % 4 == 3:
                av_block(n // 4)
        P.emit()


def pass2(P, nc, T_, npairs=4):
    with contextlib.ExitStack() as es:
        sb = lambda n, s, d: es.enter_context(nc.sbuf_tensor(n, s, d))
        ps = lambda n, s, d: es.enter_context(nc.psum_tensor(n, s, d))
        P.begin_pass()
        ident = sb('p2_id', [128, 128], BF16)
        mask = sb('p2_mask', [128, 512], BF16)
        ones_bd = sb('p2_onesbd', [128, 128], BF16)
        gA = sb('p2_gA', [128, 4], F32)
        QT = sb('p2_QT', [128, S], BF16)
        KTA = sb('p2_KTA', [128, S], BF16)
        KTB = sb('p2_KTB', [128, S], BF16)
        VT = sb('p2_VT', [128, S], BF16)
        ACC = [sb('p2_ACC%d' % i, [128, S], F32) for i in range(2)]
        DEN = sb('p2_DEN', [128, S], F32)
        yTp = sb('p2_yTp', [128, S], BF16)
        NPT = 6
        PT = [sb('p2_PT%d' % i, [128, 512], BF16) for i in range(NPT)]
        VB = [sb('p2_VB%d' % i, [128, 256], BF16) for i in range(NPT)]
        sq = [sb('p2_sq%d' % i, [128, 512], BF16) for i in range(2)]
        d2e = [sb('p2_d2e%d' % i, [128, 512], F32) for i in range(2)]
        vv = [sb('p2_vv%d' % i, [128, 512], F32) for i in range(2)]
        psS = [ps('p2_psS%d' % i, [128, 512], F32) for i in range(2)]
        psOA = [ps('p2_psOA%d' % i, [128, 512], F32) for i in range(2)]
        psOB = [ps('p2_psOB%d' % i, [128, 512], F32) for i in range(2)]
        psV = ps('p2_psV', [128, 8, 128], BF16)
        psF = ps('p2_psF', [128, 512], F32)

        P.dma('pool', ident[:], T_['ident'][:, :], 'ident', writes=['ident'], track=False)
        P.dma('pool', mask[:], T_['mask'][:, :], 'mask', writes=['mask'], track=False)
        P.dma('pool', ones_bd[:], T_['onesbd'][:, :], 'onesbd', writes=['onesbd'], track=False)
        P.dma('sp', gA[:], T_['gA'][:, :], 'gA', writes=['gA'], track=False)
        for i in range(NPT):
            P.memset('pool', VB[i][:, 64:192], 1.0, ['VBones%d' % i])

        P.memset('pool', KTA[64:128, :], 0.0, ['KTz'])
        P.memset('pool', KTB[0:64, :], 0.0, ['KTz'])
        for pr in range(npairs):
            P.dma('sp', QT[:], T_['QTd'][pr, :, :], 'QT', writes=['QT'])
            P.dma('sp', KTA[0:64, :], T_['KTd'][pr, 0:64, :], 'KTA', writes=['KT'])
            P.dma('sp', KTB[64:128, :], T_['KTd'][pr, 64:128, :], 'KTB', writes=['KT'])
            P.dma('sp', VT[:], T_['VTd'][pr, :, :], 'VT', writes=['VT'])
            blocks = []
            og = 0
            for pi, r in enumerate((1, 4, 16)):
                nb = 32 // r
                for c in range(r):
                    for b in range(nb):
                        blocks.append(dict(pi=pi, r=r, c=c, b=b, nb=nb, og=og, idx=len(blocks)))
                    og += (nb + 3) // 4

            def front(bl):
                r, c, b, nb = bl['r'], bl['c'], bl['b'], bl['nb']
                base = 128 * r * b + c
                sl_k = slice(base, base + 127 * r + 1, r)
                nq = 256 if b + 1 < nb else 128
                sl_q = slice(base, base + (nq - 1) * r + 1, r)
                n = bl['idx'] + pr * len(blocks)
                bi = n % NPT
                si = n % 2
                bl['bi'] = bi
                P.tr(psV[:, bi, :], VT[:, sl_k], ident[:], ['VT', 'ident'], ['psV'])
                P.cp('act', VB[bi][:].rearrange("p (a b) -> p a b", a=4)[:, 0::3, :],
                     psV[:, bi, :].rearrange("p (a b) -> p a b", a=2), ['psV'], ['VB%d' % bi])
                P.mm(psS[si][:, 0:nq], KTA[:, sl_k], QT[:, sl_q], True, True, ['KT', 'KTz', 'QT'], ['psS%d' % si])
                P.mm(psS[si][:, 256:256 + nq], KTB[:, sl_k], QT[:, sl_q], True, True, ['KT', 'KTz', 'QT'], ['psS%d' % si])
                if nq == 256:
                    P.act(PT[bi][:], psS[si][:], AF.Exp, ['psS%d' % si], ['PT%d' % bi], scale=0.125)
                    P.tt('pool', PT[bi][:], PT[bi][:], mask[:], ALU.mult, ['PT%d' % bi, 'mask'], ['PT%d' % bi])
                else:
                    pv = PT[bi][:].rearrange("p (a n) -> p a n", a=2)[:, :, 0:128]
                    sv = psS[si][:].rearrange("p (a n) -> p a n", a=2)[:, :, 0:128]
                    mv = mask[:].rearrange("p (a n) -> p a n", a=2)[:, :, 0:128]
                    P.act(pv, sv, AF.Exp, ['psS%d' % si], ['PT%d' % bi], scale=0.125)
                    P.tt('pool', pv, pv, mv, ALU.mult, ['PT%d' % bi, 'mask'], ['PT%d' % bi])

            def back(bl, prevbl):
                r, c, b, nb, pi = bl['r'], bl['c'], bl['b'], bl['nb'], bl['pi']
                bi = bl['bi']
                prev = prevbl['bi'] if b > 0 else None
                g, gi = divmod(b, 4)
                oi = (bl['og'] + g) % 2
                cs = slice(gi * 128, (gi + 1) * 128)
                for hh, (pso, pk, vs, qoff) in enumerate(((psOA[oi], 'psOA%d' % oi, slice(0, 128), 0),
                                                           (psOB[oi], 'psOB%d' % oi, slice(128, 256), 256))):
                    if prev is not None:
                        P.mm(pso[:, cs], VB[prev][:, vs], PT[prev][:, qoff + 128:qoff + 256], True, False,
                             ['VB%d' % prev, 'VBones%d' % prev, 'PT%d' % prev], [pk])
                    P.mm(pso[:, cs], VB[bi][:, vs], PT[bi][:, qoff:qoff + 128], prev is None, True,
                         ['VB%d' % bi, 'VBones%d' % bi, 'PT%d' % bi], [pk])
                if gi == 3 or b == nb - 1:
                    ncols = (gi + 1) * 128
                    t0 = 128 * r * (4 * g) + c
                    asl = slice(t0, t0 + (ncols - 1) * r + 1, r)
                    for hh, (pso, pk) in enumerate(((psOA[oi], 'psOA%d' % oi), (psOB[oi], 'psOB%d' % oi))):
                        if pi == 0:
                            P.cp('dve', ACC[hh][:, asl], pso[:, 0:ncols], [pk], ['ACC%d' % hh])
                        else:
                            P.tt('dve', ACC[hh][:, asl], pso[:, 0:ncols], ACC[hh][:, asl], ALU.add, [pk, 'ACC%d' % hh], ['ACC%d' % hh])

            LA = 2
            for n in range(len(blocks) + LA):
                if n < len(blocks):
                    front(blocks[n])
                m = n - LA
                if m >= 0:
                    back(blocks[m], blocks[m - 1] if m > 0 else None)
            if os.environ.get('P2_FIN', '1') == '0':
                continue
            P.dma('sp', DEN[0:64, :], ACC[0][64:128, :], 'den0', reads=['ACC0'], writes=['DEN'])
            P.dma('sp', DEN[64:128, :], ACC[1][0:64, :], 'den1', reads=['ACC1'], writes=['DEN'])
            for j in range(8):
                js = slice(j * 512, (j + 1) * 512)
                k = j % 2
                P.act(sq[k][0:64, :], ACC[0][0:64, js], AF.Square, ['ACC0'], ['sq%d' % k])
                P.act(sq[k][64:128, :], ACC[1][64:128, js], AF.Square, ['ACC1'], ['sq%d' % k])
                P.mm(psF[:], ones_bd[:], sq[k][:], True, True, ['onesbd', 'sq%d' % k], ['psF'])
                P.stt(d2e[k][:], DEN[:, js], RMS_EPS, DEN[:, js], ALU.mult, ALU.mult, ['DEN'], ['d2e%d' % k])
                P.stt(vv[k][:], psF[:], 1.0 / 64, d2e[k][:], ALU.mult, ALU.add, ['psF', 'd2e%d' % k], ['vv%d' % k])
                P.act(vv[k][:], vv[k][:], AF.Sqrt, ['vv%d' % k], ['vv%d' % k])
                P.recip(vv[k][:], vv[k][:], ['vv%d' % k], ['vv%d' % k])
                P.stt(yTp[0:64, js], ACC[0][0:64, js], gA[0:64, pr:pr + 1], vv[k][0:64, :], ALU.mult, ALU.mult,
                      ['ACC0', 'gA', 'vv%d' % k], ['yTp'])
                P.stt(yTp[64:128, js], ACC[1][64:128, js], gA[64:128, pr:pr + 1], vv[k][64:128, :], ALU.mult, ALU.mult,
                      ['ACC1', 'gA', 'vv%d' % k], ['yTp'])
            P.dma('sp', T_['YTd'][pr, :, :], yTp[:], 'oyT', reads=['yTp'])
        P.emit()


def post_norm_residual(P, psY, keys, gtab, xres, xres_keys, out_tile, out_key, st, tmp, junk, pfx):
    for hf in range(2):
        P.act(junk[:, hf * 512:(hf + 1) * 512], psY[hf][:], AF.Square, [keys[hf]], ['junk', pfx + 'ss%d' % hf], accum=st[:, hf:hf + 1])
    P.tt('dve', st[:, 2:3], st[:, 0:1], st[:, 1:2], ALU.add, [pfx + 'ss0', pfx + 'ss1'], [pfx + 'sst'])
    P.rstd(st[:, 2:3], st[:, 3:4], st[:, 4:5], 1.0 / D, RMS_EPS, pfx + 'sst')
    for hf in range(2):
        hs = slice(hf * 512, (hf + 1) * 512)
        P.stt(tmp[:, hs], psY[hf][:], st[:, 4:5], gtab[:, hs], ALU.mult, ALU.mult, [keys[hf], pfx + 'sst_r'], [pfx + 'tmp%d' % hf])
        P.tt('pool', out_tile[:, hs], tmp[:, hs], xres[:, hs], ALU.add, [pfx + 'tmp%d' % hf] + list(xres_keys), [out_key])


def pre_norm_T(P, xsrc, xkey, gtab, st, junk, hb, hbkey, psT, ident, hT, tsl, hTkey, pfx):
    P.act(junk[:], xsrc, AF.Square, [xkey], ['junk', pfx + 'ss'], accum=st[:, 8:9])
    P.rstd(st[:, 8:9], st[:, 9:10], st[:, 10:11], 1.0 / D, RMS_EPS, pfx + 'ss')
    P.stt(hb[:], xsrc, st[:, 10:11], gtab[:], ALU.mult, ALU.mult, [xkey, pfx + 'ss_r'], [hbkey])
    for dc in range(8):
        P.tr(psT[:, dc, :], hb[:, dc * 128:(dc + 1) * 128], ident[:], [hbkey, 'ident'], ['psT'])
    P.cp('act', hT[:, :, tsl], psT[:], ['psT'], [hTkey])


def pass0(P, nc, T_):
    with contextlib.ExitStack() as es:
        sb = lambda n, s, d: es.enter_context(nc.sbuf_tensor(n, s, d))
        ps = lambda n, s, d: es.enter_context(nc.psum_tensor(n, s, d))
        P.begin_pass()
        KmT, Vm = T_['KmT'], T_['Vm']
        ident = sb('p0_id', [128, 128], BF16)
        Wkv = sb('p0_wkv', [128, 8, 2048], BF16)
        gmn = sb('p0_gmn', [128, 1024], F32)
        mhT = sb('p0_mhT', [128, 8, 256], BF16)
        xt = [sb('p0_xt%d' % i, [128, 1024], F32) for i in range(2)]
        junk = sb('p0_junk', [128, 1024], BF16)
        st = sb('p0_st', [128, 16], F32)
        hb = sb('p0_hb', [128, 1024], BF16)
        psY = [ps('p0_psY%d' % i, [128, 512], F32) for i in range(2)]
        psQ = [ps('p0_psQ%d' % i, [128, 512], F32) for i in range(2)]
        psT = ps('p0_psT', [128, 8, 128], BF16)
        P.dma('pool', ident[:], T_['ident'][:, :], 'ident', writes=['ident'], track=False)
        load_w_cols(P, Wkv, T_['w_kv'], 'Wkv', [0, 1, 2, 3])
        P.dma('sp', gmn[:], T_['gt'][:, 4, :], 'g0', writes=['gmn'], track=False)
        for mt in range(2):
            P.dma('sp', xt[mt][:], T_['mem'][mt * 128:(mt + 1) * 128, :], 'xt%d' % mt, writes=['xt%d' % mt])
            pre_norm_T(P, xt[mt][:], 'xt%d' % mt, gmn, st, junk, hb, 'hb', psT, ident, mhT, slice(mt * 128, (mt + 1) * 128), 'mhT', 'm')
        for fc in range(8):
            k = fc % 2
            for dc in range(8):
                P.mm(psQ[k][:, 0:256], Wkv[:, dc, fc * 128:(fc + 1) * 128], mhT[:, dc, :], dc == 0, dc == 7, ['Wkv%d' % (fc // 4), 'mhT'], ['psQ%d' % k])
            P.cp('act', KmT[:, fc, :], psQ[k][:, 0:256], ['psQ%d' % k], ['KmT'])
        for mt in range(2):
            for hf in range(2):
                for dc in range(8):
                    P.mm(psY[hf][:], mhT[:, dc, mt * 128:(mt + 1) * 128], Wkv[:, dc, 1024 + hf * 512:1024 + (hf + 1) * 512], dc == 0, dc == 7,
                         ['Wkv%d' % (2 + hf), 'mhT'], ['psY%d' % hf])
                P.cp('act', Vm[:, mt, hf * 512:(hf + 1) * 512], psY[hf][:], ['psY%d' % hf], ['Vm'])

        P.emit()


def pass3(P, nc, T_, nblk=8):
    x = T_['x']
    with contextlib.ExitStack() as es:
        sb = lambda n, s, d: es.enter_context(nc.sbuf_tensor(n, s, d))
        ps = lambda n, s, d: es.enter_context(nc.psum_tensor(n, s, d))
        P.begin_pass()
        ident = sb('p3_id', [128, 128], BF16)
        ones = sb('p3_ones', [128, 128], BF16)
        Wout = sb('p3_wout', [128, 8, 1024], BF16)
        Wq = sb('p3_wq', [128, 8, 1024], BF16)
        Wo = sb('p3_wo', [128, 8, 1024], BF16)
        gpost = sb('p3_gpost', [128, 1024], F32)
        gpm = sb('p3_gpm', [128, 1024], F32)
        gpostm = sb('p3_gpostm', [128, 1024], F32)
        yTb = [sb('p3_yTb%d' % i, [128, 8, 512], BF16) for i in range(2)]
        xt = [sb('p3_xt%d' % i, [128, 1024], F32) for i in range(2)]
        x1 = sb('p3_x1', [128, 4, 1024], F32)
        x2 = [sb('p3_x2%d' % i, [128, 1024], F32) for i in range(2)]
        tmp = sb('p3_tmp', [128, 1024], F32)
        junk = sb('p3_junk', [128, 1024], BF16)
        st = sb('p3_st', [128, 16], F32)
        hb = sb('p3_hb', [128, 1024], BF16)
        h2T = sb('p3_h2T', [128, 8, 512], BF16)
        qT = sb('p3_qT', [128, 8, 512], BF16)
        PTm = sb('p3_PTm', [128, 8, 512], BF16)
        oT = sb('p3_oT', [128, 8, 512], BF16)
        rden = [sb('p3_rden%d' % i, [128, 512], F32) for i in range(2)]
        psY = [ps('p3_psY%d' % i, [128, 512], F32) for i in range(2)]
        psT = ps('p3_psT', [128, 8, 128], BF16)
        psQ = [ps('p3_psQ%d' % i, [128, 512], F32) for i in range(2)]
        psO = [ps('p3_psO%d' % i, [128, 512], F32) for i in range(2)]
        psD = ps('p3_psD', [128, 512], F32)

        P.dma('pool', ident[:], T_['ident'][:, :], 'ident', writes=['ident'], track=False)
        load_w_cols(P, Wout, T_['w_out'], 'Wout', [0, 1])
        load_w_cols(P, Wq, T_['w_q'], 'Wq', [0, 1])
        load_w_cols(P, Wo, T_['w_o'], 'Wo', [0, 1])
        P.memset('pool', ones[:], 1.0, ['ones'])
        KmT, Vm = T_['KmT'], T_['Vm']
        for k, (t, gi) in enumerate(((gpost, 1), (gpm, 2), (gpostm, 3))):
            P.dma('sp', t[:], T_['gt'][:, gi, :], 'g%d' % k, writes=['g%d' % gi], track=False)

        for blk in range(nblk):
            bi = blk % 2
            bs = slice(blk * 512, (blk + 1) * 512)
            P.dma('sp', yTb[bi][:], T_['YTd'][:, :, bs].rearrange("a p n -> p a n"), 'yTb%d' % bi, writes=['yTb%d' % bi])
            for t in range(4):
                T = 4 * blk + t
                i = T % 2
                tsl = slice(t * 128, (t + 1) * 128)
                P.dma('sp', xt[i][:], x[T * 128:(T + 1) * 128, :], 'xt%d' % i, writes=['xt%d' % i])
                for hf in range(2):
                    for fc in range(8):
                        P.mm(psY[hf][:], yTb[bi][:, fc, tsl], Wout[:, fc, hf * 512:(hf + 1) * 512], fc == 0, fc == 7,
                             ['yTb%d' % bi, 'Wout%d' % hf], ['psY%d' % hf])
                post_norm_residual(P, psY, ['psY0', 'psY1'], gpost, xt[i], ['xt%d' % i], x1[:, t, :], 'x1_%d' % t, st, tmp, junk, 'a')
                pre_norm_T(P, x1[:, t, :], 'x1_%d' % t, gpm, st, junk, hb, 'hb', psT, ident, h2T, tsl, 'h2T', 'b')
            for fc in range(8):
                k = fc % 2
                for dc in range(8):
                    P.mm(psQ[k][:], Wq[:, dc, fc * 128:(fc + 1) * 128], h2T[:, dc, :], dc == 0, dc == 7, ['Wq%d' % (fc // 4), 'h2T'], ['psQ%d' % k])
                P.cp('act', qT[:, fc, :], psQ[k][:], ['psQ%d' % k], ['qT%d' % fc])
            for h in range(4):
                for mt in range(2):
                    k = (h * 2 + mt) % 2
                    for cc in range(2):
                        P.mm(psQ[k][:], KmT[:, 2 * h + cc, mt * 128:(mt + 1) * 128], qT[:, 2 * h + cc, :], cc == 0, cc == 1,
                             ['KmT', 'qT%d' % (2 * h + cc)], ['psQ%d' % k])
                    P.act(PTm[:, h * 2 + mt, :], psQ[k][:], AF.Exp, ['psQ%d' % k], ['PTm%d' % h], scale=1.0 / 16)
            for h in range(4):
                for mt in range(2):
                    P.mm(psD[:], ones[:], PTm[:, h * 2 + mt, :], mt == 0, mt == 1, ['ones', 'PTm%d' % h], ['psD'])
                P.recip(rden[h % 2][:], psD[:], ['psD'], ['rden%d' % (h % 2)])
                for cc in range(2):
                    fcx = 2 * h + cc
                    for mt in range(2):
                        P.mm(psO[cc][:], Vm[:, mt, fcx * 128:(fcx + 1) * 128], PTm[:, h * 2 + mt, :], mt == 0, mt == 1,
                             ['Vm', 'PTm%d' % h], ['psO%d' % cc])
                    P.tt('dve', oT[:, fcx, :], psO[cc][:], rden[h % 2][:], ALU.mult, ['psO%d' % cc, 'rden%d' % (h % 2)], ['oT'])
            for t in range(4):
                T = 4 * blk + t
                i = T % 2
                tsl = slice(t * 128, (t + 1) * 128)
                for hf in range(2):
                    for fc in range(8):
                        P.mm(psY[hf][:], oT[:, fc, tsl], Wo[:, fc, hf * 512:(hf + 1) * 512], fc == 0, fc == 7, ['oT', 'Wo%d' % hf], ['psY%d' % hf])
                post_norm_residual(P, psY, ['psY0', 'psY1'], gpostm, x1[:, t, :], ['x1_%d' % t], x2[i], 'x2_%d' % i, st, tmp, junk, 'c')
                P.dma('sp', T_['X2d'][T * 128:(T + 1) * 128, :], x2[i][:], 'ox2_%d' % i, reads=['x2_%d' % i])
        P.emit()


def pass4(P, nc, T_, nblk=8):
    with contextlib.ExitStack() as es:
        sb = lambda n, s, d: es.enter_context(nc.sbuf_tensor(n, s, d))
        ps = lambda n, s, d: es.enter_context(nc.psum_tensor(n, s, d))
        P.begin_pass()
        ident = sb('p4_id', [128, 128], BF16)
        Wgu = sb('p4_wgu', [128, 8, 2 * FFN], BF16)
        Wd = sb('p4_wd', [128, NFC, 1024], BF16)
        gpf = sb('p4_gpf', [128, 1024], F32)
        gpostf = sb('p4_gpostf', [128, 1024], F32)
        xt = [sb('p4_xt%d' % i, [128, 1024], F32) for i in range(2)]
        ot = [sb('p4_ot%d' % i, [128, 1024], F32) for i in range(2)]
        tmp = sb('p4_tmp', [128, 1024], F32)
        junk = sb('p4_junk', [128, 1024], BF16)
        st = sb('p4_st', [128, 16], F32)
        hb = sb('p4_hb', [128, 1024], BF16)
        h3T = sb('p4_h3T', [128, 8, 512], BF16)
        actT = sb('p4_actT', [128, NFC, 512], BF16)
        sg = [sb('p4_sg%d' % i, [128, 512], F32) for i in range(2)]
        psY = [ps('p4_psY%d' % i, [128, 512], F32) for i in range(2)]
        psT = ps('p4_psT', [128, 8, 128], BF16)
        psG = [ps('p4_psG%d' % i, [128, 512], F32) for i in range(2)]
        psU = [ps('p4_psU%d' % i, [128, 512], F32) for i in range(2)]

        P.dma('pool', ident[:], T_['ident'][:, :], 'ident', writes=['ident'], track=False)
        load_w_cols(P, Wgu, T_['w_gu'], 'Wgu', [0, 5, 6, 1, 7, 2, 8, 3, 9, 4, 10])
        load_w(P, Wd, T_['w_dn'], NFC, 'Wd')
        P.dma('sp', gpf[:], T_['gt'][:, 5, :], 'g0', writes=['gpf'], track=False)
        P.dma('sp', gpostf[:], T_['gt'][:, 6, :], 'g1', writes=['gpostf'], track=False)

        for blk in range(nblk):
            for t in range(4):
                T = 4 * blk + t
                i = T % 2
                P.dma('sp', xt[i][:], T_['X2d'][T * 128:(T + 1) * 128, :], 'xt%d' % i, writes=['xt%d' % i])
                pre_norm_T(P, xt[i][:], 'xt%d' % i, gpf, st, junk, hb, 'hb', psT, ident, h3T, slice(t * 128, (t + 1) * 128), 'h3T', 'b')
            for j in range(NFC):
                k = j % 2
                for dc in range(8):
                    P.mm(psG[k][:], Wgu[:, dc, j * 128:(j + 1) * 128], h3T[:, dc, :], dc == 0, dc == 7, ['Wgu%d' % (j * 128 // 512), 'h3T'], ['psG%d' % k])
                for dc in range(8):
                    P.mm(psU[k][:], Wgu[:, dc, FFN + j * 128:FFN + (j + 1) * 128], h3T[:, dc, :], dc == 0, dc == 7, ['Wgu%d' % ((FFN + j * 128) // 512), 'h3T'], ['psU%d' % k])
                P.act(sg[k][:], psG[k][:], AF.Silu, ['psG%d' % k], ['sg%d' % k])
                P.tt('dve', actT[:, j, :], psU[k][:], sg[k][:], ALU.mult, ['psU%d' % k, 'sg%d' % k], ['actT'])
            for t in range(4):
                T = 4 * blk + t
                i = T % 2
                tsl = slice(t * 128, (t + 1) * 128)
                P.dma('sp', xt[i][:], T_['X2d'][T * 128:(T + 1) * 128, :], 'xt%d' % i, writes=['xt%d' % i])
                for hf in range(2):
                    for fc in range(NFC):
                        P.mm(psY[hf][:], actT[:, fc, tsl], Wd[:, fc, hf * 512:(hf + 1) * 512], fc == 0, fc == NFC - 1, ['actT', 'Wd'], ['psY%d' % hf])
                post_norm_residual(P, psY, ['psY0', 'psY1'], gpostf, xt[i], ['xt%d' % i], ot[i], 'ot%d' % i, st, tmp, junk, 'c')
                P.dma('sp', T_['out'][T * 128:(T + 1) * 128, :], ot[i][:], 'oo%d' % i, reads=['ot%d' % i])
        P.emit()


def host_tables():
    pos = np.arange(S, dtype=np.float32)
    invA = (np.float32(500000.0) ** (-(np.arange(0, 16, 2, dtype=np.float32) / np.float32(16)))).astype(np.float32)
    angA = (pos[:, None] * invA[None, :]).astype(np.float32).astype(np.float64)
    invR = (np.float32(10000.0) ** (-(np.arange(0, 128, 2, dtype=np.float32) / np.float32(128)))).astype(np.float32)
    angR = (pos[:, None] * invR[None, :]).astype(np.float32).astype(np.float64)

    def lay(a):
        return np.ascontiguousarray(a.reshape(NT, 128, -1).transpose(1, 0, 2)).astype(np.float32)
    tabs = {'cosA': lay(np.cos(angA)), 'sinA': lay(np.sin(angA)), 'cosR': lay(np.cos(angR)), 'sinR': lay(np.sin(angR))}
    H = 4
    log_g = np.log(1.0 - 2.0 ** (-5.0 - np.arange(H, dtype=np.float64)))
    n = np.arange(128, dtype=np.float64)
    sc = 128.0 ** -0.5
    decT = np.zeros((128, H, 128), np.float64)
    for h in range(H):
        rel = n[None, :] - n[:, None]
        decT[:, h, :] = np.where(rel >= 0, np.exp(log_g[h] * np.maximum(rel, 0.0)), 0.0) * sc
    xi = np.exp(log_g[:, None] * (n[None, :] + 1.0))
    zeta = np.exp(log_g[:, None] * (127.0 - n[None, :])) * sc
    tabs['decT'] = decT.astype(np.float32)
    tabs['xiT'] = np.ascontiguousarray(np.broadcast_to(xi[None, :, :], (128, H, 128))).astype(np.float32)
    tabs['zet'] = np.ascontiguousarray(np.broadcast_to(zeta.T[:, :, None], (128, H, 128))).astype(np.float32)
    tabs['cd'] = np.exp(log_g * 128.0)
    tabs['ident'] = np.eye(128, dtype=np.float32)
    j = np.arange(128)[:, None]
    i = np.arange(128)[None, :]
    cur = (j <= i).astype(np.float32)
    prv = (j >= i).astype(np.float32)
    tabs['mask'] = np.concatenate([cur, prv, cur, prv], axis=1)
    bd = np.zeros((128, 128), np.float32)
    bd[0:64, 0:64] = 1.0
    bd[64:128, 64:128] = 1.0
    tabs['onesbd'] = bd
    return tabs


def build(stages=(0, 1, 2, 3, 4), dbg=False, tabs=None):
    nc = bass.Bass("TRN2", target_bir_lowering=False)
    T_ = {}

    def inp(name, shape):
        T_[name] = nc.dram_tensor(name, list(shape), F32, kind="ExternalInput").ap()
    inp('x', [S, D])
    inp('mem', [256, D])
    inp('w_in', [D, 3584])
    inp('w_out', [D, D])
    inp('w_q', [D, D])
    inp('w_kv', [D, 2 * D])
    inp('w_o', [D, D])
    inp('w_gu', [D, 2 * FFN])
    inp('w_dn', [FFN, D])
    inp('gt', [128, 10, D])
    inp('gA', [128, 4])
    for nm in ('cosA', 'sinA'):
        inp(nm, [128, 32, 8])
    for nm in ('cosR', 'sinR'):
        inp(nm, [128, 32, 64])
    for nm in ('decT', 'xiT', 'zet'):
        inp(nm, [128, 4, 128])
    inp('ident', [128, 128])
    inp('mask', [128, 512])
    inp('onesbd', [128, 128])
    T_['cd'] = tabs['cd']
    T_['out'] = nc.dram_tensor('out', [S, D], F32, kind="ExternalOutput").ap()
    kind = "ExternalOutput" if dbg else "Internal"
    T_['QTd'] = nc.dram_tensor('QTd', [4, 128, S], BF16, kind=kind).ap()
    T_['KTd'] = nc.dram_tensor('KTd', [4, 128, S], BF16, kind=kind).ap()
    T_['VTd'] = nc.dram_tensor('VTd', [4, 128, S], BF16, kind=kind).ap()
    T_['YTd'] = nc.dram_tensor('YTd', [8, 128, S], BF16, kind=kind).ap()
    T_['X2d'] = nc.dram_tensor('X2d', [S, D], F32, kind=kind).ap()
    with contextlib.ExitStack() as es:
        P = Prog(nc, es)
        T_['KmT'] = es.enter_context(nc.sbuf_tensor('KmT', [128, 8, 256], BF16))
        T_['Vm'] = es.enter_context(nc.sbuf_tensor('Vm', [128, 2, 1024], BF16))
        if 0 in stages:
            pass0(P, nc, T_)
        if 1 in stages:
            pass1(P, nc, T_)
        if 2 in stages:
            pass2(P, nc, T_)
        if 3 in stages:
            pass3(P, nc, T_)
        if 4 in stages:
            pass4(P, nc, T_)
    return nc


def make_inputs(x, mem, pre_mix_g, post_mix_g, w_in, attn_gn_g, ret_gn_g, w_out,
                pre_mem_g, post_mem_g, mem_norm_g, w_q_mem, w_kv_mem, w_o_mem,
                pre_ffn_g, post_ffn_g, w_gate_up, w_down, tabs):
    f = lambda a: np.ascontiguousarray(np.asarray(a, dtype=np.float32))
    gt = np.zeros((128, 10, D), np.float32)
    for k, g in enumerate((pre_mix_g, post_mix_g, pre_mem_g, post_mem_g, mem_norm_g, pre_ffn_g, post_ffn_g)):
        gt[:, k, :] = np.broadcast_to(f(g)[0][None, :], (128, D))
    gt[:, 9, 0:512] = np.broadcast_to(f(ret_gn_g)[0][None, :], (128, 512))
    gA = np.ascontiguousarray(f(attn_gn_g)[0].reshape(4, 128).T)
    shared = {
        'w_in': f(w_in)[0], 'w_out': f(w_out)[0], 'w_q': f(w_q_mem)[0], 'w_kv': f(w_kv_mem)[0], 'w_o': f(w_o_mem)[0],
        'w_gu': f(w_gate_up)[0], 'w_dn': f(w_down)[0], 'gt': gt, 'gA': gA,
    }
    for nm in ('cosA', 'sinA', 'cosR', 'sinR', 'decT', 'xiT', 'zet', 'ident', 'mask', 'onesbd'):
        shared[nm] = tabs[nm]
    xs = f(x)
    ms = f(mem)
    return [dict(shared, x=xs[b], mem=ms[b]) for b in range(xs.shape[0])]


def kernel(**inputs):
    tabs = host_tables()
    in_maps = make_inputs(tabs=tabs, **inputs)
    nc = build(tabs=tabs)
    res = run_bass_kernel_spmd(nc, in_maps, core_ids=list(range(8)))
    return np.stack([np.asarray(r['out'], dtype=np.float32) for r in res.results], axis=0)
```

```python
import contextlib
import os
import numpy as np
import concourse.bass as bass
import concourse.mybir as mybir
from concourse.bass_utils import run_bass_kernel_spmd

F32 = mybir.dt.float32
BF16 = mybir.dt.bfloat16
AF = mybir.ActivationFunctionType
ALU = mybir.AluOpType

S = 4096
D = 1024
NT = 32
RMS_EPS = 1e-6
GN_EPS = 1e-5
FFN = 2816
NFC = 22


class Prog:
    ENG = ('pe', 'act', 'dve', 'pool', 'sp')
    LIM = 16000

    def __init__(self, nc, es):
        self.nc = nc
        self.es = es
        self.esem = {e: [] for e in ('pe', 'act', 'dve', 'pool')}
        self.dsem = {}
        self.sigcount = {e: 0 for e in ('pe', 'act', 'dve', 'pool')}
        self.dma_cnt = {}
        self.begin_pass()

    def begin_pass(self):
        self.q = {e: [] for e in self.ENG}
        self.last_w = {}
        self.readers = {}
        self.waited = {e: {} for e in self.ENG}
        self.pass_dma = {}

    def _dep(self, ins, sig):
        if sig is None:
            return
        qn = ins['q']
        kind, ref, val = sig
        if kind == 'eng' and ref == 'pe' and qn == 'pe':
            return
        if self.waited[qn].get((kind, ref), -1) >= val:
            return
        self.waited[qn][(kind, ref)] = val
        ins['waits'].append(sig)
        if kind == 'eng':
            self.q[ref][val]['signal'] = True

    def add(self, qn, fn, reads=(), writes=(), dma_slot=None, track=True):
        ins = {'q': qn, 'fn': fn, 'waits': [], 'signal': False, 'dma_slot': dma_slot}
        if dma_slot is None:
            sig = ('eng', qn, len(self.q[qn]))
        else:
            c = self.dma_cnt.get(dma_slot, 0) + 1
            assert c < 2000, dma_slot
            self.dma_cnt[dma_slot] = c
            sig = ('dma', dma_slot, 16 * c)
            self.pass_dma[dma_slot] = 16 * c
        ins['sig'] = sig
        if track:
            for k in reads:
                self._dep(ins, self.last_w.get(k))
                if k.startswith('ps'):
                    for r in self.readers.get(k, ()):
                        if not (r[0] == 'eng' and r[1] == qn):
                            self._dep(ins, r)
            for k in writes:
                self._dep(ins, self.last_w.get(k))
                for r in self.readers.get(k, ()):
                    self._dep(ins, r)
        for k in reads:
            self.readers.setdefault(k, []).append(sig)
        for k in writes:
            self.last_w[k] = sig
            self.readers[k] = []
        self.q[qn].append(ins)
        return sig

    def _esem(self, e, idx):
        while len(self.esem[e]) <= idx:
            self.esem[e].append(self.es.enter_context(self.nc.semaphore('s_%s%d' % (e, len(self.esem[e])))))
        return self.esem[e][idx]

    def _dsem(self, slot):
        if slot not in self.dsem:
            self.dsem[slot] = self.es.enter_context(self.nc.semaphore('d_%d' % len(self.dsem)))
        return self.dsem[slot]

    def emit(self):
        nc = self.nc
        for e in self.esem:
            c = self.sigcount[e]
            for ins in self.q[e]:
                if ins['dma_slot'] is None and ins['signal']:
                    ins['semidx'] = c // self.LIM
                    ins['sigval'] = c % self.LIM + 1
                    c += 1
            self.sigcount[e] = c

        def resolve(sig):
            kind, ref, val = sig
            if kind == 'eng':
                i = self.q[ref][val]
                return self._esem(ref, i['semidx']), i['sigval']
            return self._dsem(ref), val

        final = [('dma', s, v) for s, v in self.pass_dma.items()]
        for e in self.ENG:
            for ins in self.q[e]:
                for w in ins['waits']:
                    resolve(w)
                if ins['dma_slot'] is not None:
                    self._dsem(ins['dma_slot'])
                elif ins['signal']:
                    self._esem(e, ins['semidx'])

        def run(e, eng):
            for ins in self.q[e]:
                for w in ins['waits']:
                    s, v = resolve(w)
                    eng.wait_ge(s, v)
                bi = ins['fn'](eng)
                if ins['dma_slot'] is not None:
                    bi.then_inc(self._dsem(ins['dma_slot']), 16)
                elif ins['signal']:
                    bi.then_inc(self._esem(e, ins['semidx']), 1)
            if e == 'sp':
                for sg in final:
                    s, v = resolve(sg)
                    eng.wait_ge(s, v)

        with nc.Block() as block:
            @block.tensor
            def _(eng):
                run('pe', eng)

            @block.scalar
            def _(eng):
                run('act', eng)

            @block.vector
            def _(eng):
                run('dve', eng)

            @block.gpsimd
            def _(eng):
                run('pool', eng)

            @block.sync
            def _(eng):
                run('sp', eng)

    def mm(self, out, lhsT, rhs, start, stop, reads, writes):
        return self.add('pe', lambda e: e.matmul(out, lhsT=lhsT, rhs=rhs, start=start, stop=stop), reads, writes)

    def tr(self, out, in_, ident, reads, writes):
        return self.add('pe', lambda e: e.transpose(out=out, in_=in_, identity=ident), reads, writes)

    def act(self, out, in_, func, reads, writes, scale=None, accum=None):
        def f(e):
            kw = {}
            if scale is not None:
                kw['scale'] = scale
            if accum is not None:
                kw['accum_out'] = accum
            return e.activation(out=out, in_=in_, func=func, **kw)
        return self.add('act', f, reads, writes)

    def tt(self, q, out, in0, in1, op, reads, writes):
        return self.add(q, lambda e: e.tensor_tensor(out=out, in0=in0, in1=in1, op=op), reads, writes)

    def ts(self, q, out, in0, s1, s2, op0, op1, reads, writes):
        if s2 is None:
            return self.add(q, lambda e: e.tensor_scalar(out=out, in0=in0, scalar1=s1, scalar2=None, op0=op0), reads, writes)
        return self.add(q, lambda e: e.tensor_scalar(out=out, in0=in0, scalar1=s1, scalar2=s2, op0=op0, op1=op1), reads, writes)

    def stt(self, out, in0, scalar, in1, op0, op1, reads, writes):
        return self.add('dve', lambda e: e.scalar_tensor_tensor(out=out, in0=in0, scalar=scalar, in1=in1, op0=op0, op1=op1), reads, writes)

    def cp(self, q, out, in_, reads, writes):
        if q == 'act':
            return self.add('act', lambda e: e.copy(out=out, in_=in_), reads, writes)
        return self.add(q, lambda e: e.tensor_copy(out=out, in_=in_), reads, writes)

    def recip(self, out, in_, reads, writes):
        return self.add('dve', lambda e: e.reciprocal(out=out, in_=in_), reads, writes)

    def dma(self, q, out, in_, slot, reads=(), writes=(), track=True):
        return self.add(q, lambda e: e.dma_start(out=out, in_=in_), reads, writes, dma_slot=slot, track=track)

    def memset(self, q, ap, val, writes):
        return self.add(q, lambda e: e.memset(ap, val), (), writes)

    def rstd(self, ss_ap, tmp_ap, out_ap, mult, eps, key):
        self.ts('dve', tmp_ap, ss_ap, mult, eps, ALU.mult, ALU.add, [key], [key + '_t'])
        self.act(tmp_ap, tmp_ap, AF.Sqrt, [key + '_t'], [key + '_t'])
        self.recip(out_ap, tmp_ap, [key + '_t'], [key + '_r'])


def load_w(P, dst, src, nchunk, key):
    for dc in range(nchunk):
        P.dma('pool', dst[:, dc, :], src[dc * 128:(dc + 1) * 128, :], slot=key, writes=[key], track=False)


def load_w_cols(P, dst, src, key, groups):
    for g in groups:
        cs = slice(g * 512, (g + 1) * 512)
        P.dma('pool', dst[:, :, cs], src[:, cs].rearrange("(dc p) n -> p dc n", p=128), slot='%s%d' % (key, g),
              writes=['%s%d' % (key, g)], track=False)


def pass1(P, nc, T_, dbg_tiles=NT):
    x, w_in = T_['x'], T_['w_in']
    with contextlib.ExitStack() as es:
        sb = lambda n, s, d: es.enter_context(nc.sbuf_tensor(n, s, d))
        ps = lambda n, s, d: es.enter_context(nc.psum_tensor(n, s, d))
        P.begin_pass()
        W = sb('p1_w', [128, 8, 3584], BF16)
        ident = sb('p1_id', [128, 128], BF16)
        cosR = sb('p1_cosR', [128, 32, 64], F32)
        sinR = sb('p1_sinR', [128, 32, 64], F32)
        cosA = sb('p1_cosA', [128, 32, 8], F32)
        sinA = sb('p1_sinA', [128, 32, 8], F32)
        decT = sb('p1_decT', [128, 4, 128], F32)
        xiT = sb('p1_xiT', [128, 4, 128], F32)
        zet = sb('p1_zet', [128, 4, 128], F32)
        gpre = sb('p1_gpre', [128, 1024], F32)
        gret = sb('p1_gret', [128, 512], F32)
        xt = [sb('p1_xt%d' % i, [128, 1024], F32) for i in range(2)]
        junk = sb('p1_junk', [128, 1024], BF16)
        st = sb('p1_st', [128, 16], F32)
        hb = [sb('p1_hb%d' % i, [128, 1024], BF16) for i in range(2)]
        hTb = [sb('p1_hT%d' % i, [128, 8, 512], BF16) for i in range(2)]
        tA = sb('p1_tA', [128, 4, 64], F32)
        qka = [sb('p1_qka%d' % i, [128, 2, 512], BF16) for i in range(2)]
        aqkT = [sb('p1_aqkT%d' % i, [128, 2, 4, 512], BF16) for i in range(2)]
        avT = [sb('p1_avT%d' % i, [128, 4, 512], BF16) for i in range(2)]
        yTr = [sb('p1_yTr%d' % i, [128, 4, 512], BF16) for i in range(2)]
        tR = sb('p1_tR', [128, 4, 256], F32)
        qkr = [sb('p1_qkr%d' % i, [128, 2, 512], BF16) for i in range(2)]
        kz = [sb('p1_kz%d' % i, [128, 4, 128], BF16) for i in range(2)]
        QKT = [sb('p1_QKT%d' % i, [128, 8, 128], BF16) for i in range(2)]
        QTxi = [sb('p1_QTxi%d' % i, [128, 4, 128], BF16) for i in range(2)]
        vb = [sb('p1_vb%d' % i, [128, 512], BF16) for i in range(2)]
        gs = sb('p1_gs', [128, 512], F32)
        gs2 = [sb('p1_gs2%d' % i, [128, 512], F32) for i in range(2)]
        innT = sb('p1_innT', [128, 4, 128], BF16)
        R = sb('p1_R', [128, 4, 128], F32)
        Rbf = sb('p1_Rbf', [128, 4, 128], BF16)
        osb = sb('p1_osb', [128, 512], F32)
        rn = sb('p1_rn', [128, 512], F32)
        rb = [sb('p1_rb%d' % i, [128, 512], BF16) for i in range(2)]
        gst = sb('p1_gst', [128, 32], F32)
        psQa = ps('p1_psQa', [128, 512], F32)
        psKa = ps('p1_psKa', [128, 512], F32)
        psQr = ps('p1_psQr', [128, 512], F32)
        psKr = ps('p1_psKr', [128, 512], F32)
        psVG = ps('p1_psVG', [128, 512], F32)
        psT = ps('p1_psT', [128, 8, 128], BF16)
        psS = ps('p1_psS', [128, 512], F32)
        psO = ps('p1_psO', [128, 512], F32)

        load_w_cols(P, W, w_in, 'W', [0, 1, 3, 4, 5, 6, 2])
        P.dma('pool', ident[:], T_['ident'][:, :], 'ident', writes=['ident'], track=False)
        for nm, t in (('cosR', cosR), ('sinR', sinR), ('cosA', cosA), ('sinA', sinA), ('decT', decT), ('xiT', xiT), ('zet', zet)):
            P.dma('sp', t[:], T_[nm][:, :, :], nm, writes=[nm], track=False)
        P.dma('sp', gpre[:], T_['gt'][:, 0, :], 'gpre', writes=['gpre'], track=False)
        P.dma('sp', gret[:], T_['gt'][:, 9, 0:512], 'gret', writes=['gret'], track=False)
        P.memset('pool', R[:], 0.0, ['R'])

        cd = T_['cd']

        def xload(n):
            i = n % 2
            P.dma('sp', xt[i][:], x[n * 128:(n + 1) * 128, :], 'xt%d' % i, writes=['xt%d' % i])

        def s0_chain(n):
            i = n % 2
            P.act(junk[:], xt[i][:], AF.Square, ['xt%d' % i], ['junk', 'ss'], accum=st[:, 0:1])
            P.rstd(st[:, 0:1], st[:, 1:2], st[:, 2:3], 1.0 / D, RMS_EPS, 'ss')
            P.stt(hb[i][:], xt[i][:], st[:, 2:3], gpre[:], ALU.mult, ALU.mult, ['xt%d' % i, 'ss_r', 'gpre'], ['hb%d' % i])

        def s0_pe(n):
            i = n % 2
            bi, t = (n // 4) % 2, n % 4
            for dc in range(8):
                P.tr(psT[:, dc, :], hb[i][:, dc * 128:(dc + 1) * 128], ident[:], ['hb%d' % i, 'ident'], ['psT'])
            P.cp('act', hTb[bi][:, :, t * 128:(t + 1) * 128], psT[:], ['psT'], ['hT%d_%d' % (bi, t)])

        def proj(n, psum, c0, key):
            bi, t = (n // 4) % 2, n % 4
            for dc in range(8):
                P.mm(psum[:], hTb[bi][:, dc, t * 128:(t + 1) * 128], W[:, dc, c0:c0 + 512], dc == 0, dc == 7,
                     ['hT%d_%d' % (bi, t), 'W%d' % (c0 // 512)], [key])

        def s1_att(n):
            i = n % 2
            proj(n, psQa, 0, 'psQa')
            proj(n, psKa, 512, 'psKa')
            cb = cosA[:, n, :].unsqueeze(1).broadcast_to([128, 8, 8])
            snb = sinA[:, n, :].unsqueeze(1).broadcast_to([128, 8, 8])
            for j, (pp, pk) in enumerate(((psQa, 'psQa'), (psKa, 'psKa'))):
                v = pp[:].rearrange("p (h d) -> p h d", h=8)
                x1, x2 = v[:, :, 0:8], v[:, :, 8:16]
                tv = [tA[:, k, :].rearrange("p (h d) -> p h d", h=8) for k in range(4)]
                P.tt('dve', tv[0], x1, cb, ALU.mult, [pk, 'cosA'], ['tA0'])
                P.tt('dve', tv[1], x2, snb, ALU.mult, [pk, 'sinA'], ['tA1'])
                P.tt('dve', tv[2], x2, cb, ALU.mult, [pk, 'cosA'], ['tA2'])
                P.tt('dve', tv[3], x1, snb, ALU.mult, [pk, 'sinA'], ['tA3'])
                o = qka[i][:, j, :].rearrange("p (h d) -> p h d", h=8)
                P.tt('pool', o[:, :, 0:8], tv[0], tv[1], ALU.subtract, ['tA0', 'tA1'], ['qka%d' % i])
                P.tt('pool', o[:, :, 8:16], tv[2], tv[3], ALU.add, ['tA2', 'tA3'], ['qka%d' % i])
                P.cp('act', o[:, :, 16:64], v[:, :, 16:64], [pk], ['qka%d' % i])

        def s2_att_tr(n):
            i = n % 2
            bi, t = (n // 4) % 2, n % 4
            for j in range(2):
                for pr in range(4):
                    P.tr(psT[:, j * 4 + pr, :], qka[i][:, j, pr * 128:(pr + 1) * 128], ident[:], ['qka%d' % i, 'ident'], ['psT'])
            P.cp('act', aqkT[bi][:, :, :, t * 128:(t + 1) * 128], psT[:].rearrange("p (a b) c -> p a b c", a=2), ['psT'], ['aqkT%d' % bi])

        def s1_ret(n):
            i = n % 2
            proj(n, psQr, 1536, 'psQr')
            proj(n, psKr, 2048, 'psKr')
            cb = cosR[:, n, :].unsqueeze(1).broadcast_to([128, 4, 64])
            snb = sinR[:, n, :].unsqueeze(1).broadcast_to([128, 4, 64])
            for j, (pp, pk) in enumerate(((psQr, 'psQr'), (psKr, 'psKr'))):
                v = pp[:].rearrange("p (h d) -> p h d", h=4)
                x1, x2 = v[:, :, 0:64], v[:, :, 64:128]
                tv = [tR[:, k, :].rearrange("p (h d) -> p h d", h=4) for k in range(4)]
                P.tt('dve', tv[0], x1, cb, ALU.mult, [pk, 'cosR'], ['tR0'])
                P.tt('dve', tv[1], x2, snb, ALU.mult, [pk, 'sinR'], ['tR1'])
                P.tt('dve', tv[2], x2, cb, ALU.mult, [pk, 'cosR'], ['tR2'])
                P.tt('dve', tv[3], x1, snb, ALU.mult, [pk, 'sinR'], ['tR3'])
                o = qkr[i][:, j, :].rearrange("p (h d) -> p h d", h=4)
                P.tt('pool', o[:, :, 0:64], tv[0], tv[1], ALU.subtract, ['tR0', 'tR1'], ['qkr%d' % i])
                P.tt('pool', o[:, :, 64:128], tv[2], tv[3], ALU.add, ['tR2', 'tR3'], ['qkr%d' % i])
            P.tt('pool', kz[i][:], qkr[i][:, 1, :].rearrange("p (h d) -> p h d", h=4), zet[:], ALU.mult, ['qkr%d' % i, 'zet'], ['kz%d' % i])

        def s2_ret_tr(n):
            i = n % 2
            for j in range(2):
                for h in range(4):
                    P.tr(psT[:, j * 4 + h, :], qkr[i][:, j, h * 128:(h + 1) * 128], ident[:], ['qkr%d' % i, 'ident'], ['psT'])
            P.cp('act', QKT[i][:], psT[:], ['psT'], ['QKT%d' % i])
            P.tt('dve', QTxi[i][:], psT[:, 0:4, :], xiT[:], ALU.mult, ['psT', 'xiT'], ['QTxi%d' % i])

        def s1_v(n):
            i = n % 2
            proj(n, psVG, 2560, 'psVG')
            P.cp('act', vb[i][:], psVG[:], ['psVG'], ['vb%d' % i])

        def s1_g(n):
            i = n % 2
            proj(n, psVG, 3072, 'psVG')
            P.act(gs[:], psVG[:], AF.Silu, ['psVG'], ['gs'])
            P.tt('pool', gs2[i][:], gs[:], gret[:], ALU.mult, ['gs', 'gret'], ['gs2%d' % i])

        def s2_scores(n):
            i = n % 2
            for h in range(4):
                hs = slice(h * 128, (h + 1) * 128)
                P.mm(psS[:, hs], QKT[i][:, 4 + h, :], QKT[i][:, h, :], True, True, ['QKT%d' % i], ['psS'])
            P.tt('dve', innT[:], psS[:].rearrange("p (h n) -> p h n", h=4), decT[:], ALU.mult, ['psS', 'decT'], ['innT'])

        def s2_out(n):
            i = n % 2
            for h in range(4):
                hs = slice(h * 128, (h + 1) * 128)
                P.mm(psO[:, hs], innT[:, h, :], vb[i][:, hs], True, n == 0, ['innT', 'vb%d' % i], ['psO'])
                if n > 0:
                    P.mm(psO[:, hs], QTxi[i][:, h, :], Rbf[:, h, :], False, True, ['QTxi%d' % i, 'Rbf'], ['psO'])
            if n < NT - 1:
                for h in range(4):
                    hs = slice(h * 128, (h + 1) * 128)
                    P.mm(psS[:, hs], kz[i][:, h, :], vb[i][:, hs], True, True, ['kz%d' % i, 'vb%d' % i], ['psS'])
                for h in range(4):
                    hs = slice(h * 128, (h + 1) * 128)
                    P.stt(R[:, h, :], R[:, h, :], float(cd[h]), psS[:, hs], ALU.mult, ALU.add, ['R', 'psS'], ['R'])
                P.cp('act', Rbf[:], R[:], ['R'], ['Rbf'])
            for h in range(4):
                hs = slice(h * 128, (h + 1) * 128)
                P.act(osb[:, hs], psO[:, hs], AF.Copy, ['psO'], ['osb', 'gst_s'], accum=gst[:, h:h + 1])
            for h in range(4):
                hs = slice(h * 128, (h + 1) * 128)
                P.act(junk[:, hs], psO[:, hs], AF.Square, ['psO'], ['junk', 'gst_q'], accum=gst[:, 4 + h:5 + h])
            P.ts('dve', gst[:, 8:12], gst[:, 0:4], 1.0 / 128, None, ALU.mult, None, ['gst_s'], ['gst_m'])
            P.tt('dve', gst[:, 12:16], gst[:, 8:12], gst[:, 8:12], ALU.mult, ['gst_m'], ['gst_m2'])
            P.stt(gst[:, 16:20], gst[:, 4:8], 1.0 / 128, gst[:, 12:16], ALU.mult, ALU.subtract, ['gst_q', 'gst_m2'], ['gst_v'])
            P.ts('dve', gst[:, 20:24], gst[:, 16:20], GN_EPS, None, ALU.add, None, ['gst_v'], ['gst_ve'])
            P.act(gst[:, 20:24], gst[:, 20:24], AF.Sqrt, ['gst_ve'], ['gst_ve'])
            P.recip(gst[:, 24:28], gst[:, 20:24], ['gst_ve'], ['gst_r'])
            for h in range(4):
                hs = slice(h * 128, (h + 1) * 128)
                P.ts('dve', rn[:, hs], osb[:, hs], gst[:, 8 + h:9 + h], gst[:, 24 + h:25 + h], ALU.subtract, ALU.mult,
                     ['osb', 'gst_m', 'gst_r'], ['rn'])
            P.tt('pool', rb[i][:], rn[:], gs2[i][:], ALU.mult, ['rn', 'gs2%d' % i], ['rb%d' % i])

        def s3_r_tr(n):
            i = n % 2
            bi, t = (n // 4) % 2, n % 4
            for h in range(4):
                P.tr(psT[:, h, :], rb[i][:, h * 128:(h + 1) * 128], ident[:], ['rb%d' % i, 'ident'], ['psT'])
            P.cp('act', yTr[bi][:, :, t * 128:(t + 1) * 128], psT[:, 0:4, :], ['psT'], ['yTr%d' % bi])

        def av_block(blk):
            bi = blk % 2
            hks = ['hT%d_%d' % (bi, t) for t in range(4)]
            for pr in range(4):
                for dc in range(8):
                    P.mm(psVG[:], W[:, dc, 1024 + pr * 128:1024 + (pr + 1) * 128], hTb[bi][:, dc, :], dc == 0, dc == 7, hks + ['W2'], ['psVG'])
                P.cp('act', avT[bi][:, pr, :], psVG[:], ['psVG'], ['avT%d' % bi])
            bs = slice(blk * 512, (blk + 1) * 512)
            P.dma('sp', T_['VTd'][:, :, bs].rearrange("a p n -> p a n"), avT[bi][:], 'ov%d' % bi, reads=['avT%d' % bi])

        def ok(n):
            return 0 <= n < NT

        xload(0)
        xload(1)
        s0_chain(0)
        s0_pe(0)
        for n in range(0, NT + 2):
            if ok(n + 2):
                xload(n + 2)
            if ok(n + 1):
                s0_chain(n + 1)
            if ok(n):
                s1_att(n)
            if ok(n - 1):
                s2_att_tr(n - 1)
                if (n - 1) % 4 == 3:
                    blk = (n - 1) // 4
                    bs = slice(blk * 512, (blk + 1) * 512)
                    P.dma('sp', T_['QTd'][:, :, bs].rearrange("a p n -> p a n"), aqkT[blk % 2][:, 0, :, :], 'oq%d' % (blk % 2), reads=['aqkT%d' % (blk % 2)])
                    P.dma('sp', T_['KTd'][:, :, bs].rearrange("a p n -> p a n"), aqkT[blk % 2][:, 1, :, :], 'ok%d' % (blk % 2), reads=['aqkT%d' % (blk % 2)])
            if ok(n):
                s1_ret(n)
            if ok(n - 1):
                s2_ret_tr(n - 1)
            if ok(n):
                s1_v(n)
            if ok(n - 1):
                s2_scores(n - 1)
            if ok(n):
                s1_g(n)
            if ok(n - 1):
                s2_out(n - 1)
            if ok(n - 2):
                s3_r_tr(n - 2)
                if (n - 2) % 4 == 3:
                    blk = (n - 2) // 4
                    bs = slice(blk * 512, (blk + 1) * 512)
                    P.dma('sp', T_['YTd'][4:8, :, bs].rearrange("a p n -> p a n"), yTr[blk % 2][:], 'oy%d' % (blk % 2), reads=['yTr%d' % (blk % 2)])
            if ok(n + 1):
                s0_pe(n + 1)
            if ok(n) and n % 4 == 3:
                av_block(n // 4)
        P.emit()


def pass2(P, nc, T_, npairs=4):
    with contextlib.ExitStack() as es:
        sb = lambda n, s, d: es.enter_context(nc.sbuf_tensor(n, s, d))
        ps = lambda n, s, d: es.enter_context(nc.psum_tensor(n, s, d))
        P.begin_pass()
        ident = sb('p2_id', [128, 128], BF16)
        mask = sb('p2_mask', [128, 512], BF16)
        ones_bd = sb('p2_onesbd', [128, 128], BF16)
        gA = sb('p2_gA', [128, 4], F32)
        QT = sb('p2_QT', [128, S], BF16)
        KTA = sb('p2_KTA', [128, S], BF16)
        KTB = sb('p2_KTB', [128, S], BF16)
        VT = sb('p2_VT', [128, S], BF16)
        ACC = [sb('p2_ACC%d' % i, [128, S], F32) for i in range(2)]
        DEN = sb('p2_DEN', [128, S], F32)
        yTp = sb('p2_yTp', [128, S], BF16)
        NPT = 6
        PT = [sb('p2_PT%d' % i, [128, 512], BF16) for i in range(NPT)]
        VB = [sb('p2_VB%d' % i, [128, 256], BF16) for i in range(NPT)]
        sq = [sb('p2_sq%d' % i, [128, 512], BF16) for i in range(2)]
        d2e = [sb('p2_d2e%d' % i, [128, 512], F32) for i in range(2)]
        vv = [sb('p2_vv%d' % i, [128, 512], F32) for i in range(2)]
        psS = [ps('p2_psS%d' % i, [128, 512], F32) for i in range(2)]
        psOA = [ps('p2_psOA%d' % i, [128, 512], F32) for i in range(2)]
        psOB = [ps('p2_psOB%d' % i, [128, 512], F32) for i in range(2)]
        psV = ps('p2_psV', [128, 8, 128], BF16)
        psF = ps('p2_psF', [128, 512], F32)

        P.dma('pool', ident[:], T_['ident'][:, :], 'ident', writes=['ident'], track=False)
        P.dma('pool', mask[:], T_['mask'][:, :], 'mask', writes=['mask'], track=False)
        P.dma('pool', ones_bd[:], T_['onesbd'][:, :], 'onesbd', writes=['onesbd'], track=False)
        P.dma('sp', gA[:], T_['gA'][:, :], 'gA', writes=['gA'], track=False)
        for i in range(NPT):
            P.memset('pool', VB[i][:, 64:192], 1.0, ['VBones%d' % i])

        P.memset('pool', KTA[64:128, :], 0.0, ['KTz'])
        P.memset('pool', KTB[0:64, :], 0.0, ['KTz'])
        for pr in range(npairs):
            P.dma('sp', QT[:], T_['QTd'][pr, :, :], 'QT', writes=['QT'])
            P.dma('sp', KTA[0:64, :], T_['KTd'][pr, 0:64, :], 'KTA', writes=['KT'])
            P.dma('sp', KTB[64:128, :], T_['KTd'][pr, 64:128, :], 'KTB', writes=['KT'])
            P.dma('sp', VT[:], T_['VTd'][pr, :, :], 'VT', writes=['VT'])
            blocks = []
            og = 0
            for pi, r in enumerate((1, 4, 16)):
                nb = 32 // r
                for c in range(r):
                    for b in range(nb):
                        blocks.append(dict(pi=pi, r=r, c=c, b=b, nb=nb, og=og, idx=len(blocks)))
                    og += (nb + 3) // 4

            def front(bl):
                r, c, b, nb = bl['r'], bl['c'], bl['b'], bl['nb']
                base = 128 * r * b + c
                sl_k = slice(base, base + 127 * r + 1, r)
                nq = 256 if b + 1 < nb else 128
                sl_q = slice(base, base + (nq - 1) * r + 1, r)
                n = bl['idx'] + pr * len(blocks)
                bi = n % NPT
                si = n % 2
                bl['bi'] = bi
                P.tr(psV[:, bi, :], VT[:, sl_k], ident[:], ['VT', 'ident'], ['psV'])
                P.cp('act', VB[bi][:].rearrange("p (a b) -> p a b", a=4)[:, 0::3, :],
                     psV[:, bi, :].rearrange("p (a b) -> p a b", a=2), ['psV'], ['VB%d' % bi])
                P.mm(psS[si][:, 0:nq], KTA[:, sl_k], QT[:, sl_q], True, True, ['KT', 'KTz', 'QT'], ['psS%d' % si])
                P.mm(psS[si][:, 256:256 + nq], KTB[:, sl_k], QT[:, sl_q], True, True, ['KT', 'KTz', 'QT'], ['psS%d' % si])
                if nq == 256:
                    P.act(PT[bi][:], psS[si][:], AF.Exp, ['psS%d' % si], ['PT%d' % bi], scale=0.125)
                    P.tt('pool', PT[bi][:], PT[bi][:], mask[:], ALU.mult, ['PT%d' % bi, 'mask'], ['PT%d' % bi])
                else:
                    pv = PT[bi][:].rearrange("p (a n) -> p a n", a=2)[:, :, 0:128]
                    sv = psS[si][:].rearrange("p (a n) -> p a n", a=2)[:, :, 0:128]
                    mv = mask[:].rearrange("p (a n) -> p a n", a=2)[:, :, 0:128]
                    P.act(pv, sv, AF.Exp, ['psS%d' % si], ['PT%d' % bi], scale=0.125)
                    P.tt('pool', pv, pv, mv, ALU.mult, ['PT%d' % bi, 'mask'], ['PT%d' % bi])

            def back(bl, prevbl):
                r, c, b, nb, pi = bl['r'], bl['c'], bl['b'], bl['nb'], bl['pi']
                bi = bl['bi']
                prev = prevbl['bi'] if b > 0 else None
                g, gi = divmod(b, 4)
                oi = (bl['og'] + g) % 2
                cs = slice(gi * 128, (gi + 1) * 128)
                for hh, (pso, pk, vs, qoff) in enumerate(((psOA[oi], 'psOA%d' % oi, slice(0, 128), 0),
                                                           (psOB[oi], 'psOB%d' % oi, slice(128, 256), 256))):
                    if prev is not None:
                        P.mm(pso[:, cs], VB[prev][:, vs], PT[prev][:, qoff + 128:qoff + 256], True, False,
                             ['VB%d' % prev, 'VBones%d' % prev, 'PT%d' % prev], [pk])
                    P.mm(pso[:, cs], VB[bi][:, vs], PT[bi][:, qoff:qoff + 128], prev is None, True,
                         ['VB%d' % bi, 'VBones%d' % bi, 'PT%d' % bi], [pk])
                if gi == 3 or b == nb - 1:
                    ncols = (gi + 1) * 128
                    t0 = 128 * r * (4 * g) + c
                    asl = slice(t0, t0 + (ncols - 1) * r + 1, r)
                    for hh, (pso, pk) in enumerate(((psOA[oi], 'psOA%d' % oi), (psOB[oi], 'psOB%d' % oi))):
                        if pi == 0:
                            P.cp('dve', ACC[hh][:, asl], pso[:, 0:ncols], [pk], ['ACC%d' % hh])
                        else:
                            P.tt('dve', ACC[hh][:, asl], pso[:, 0:ncols], ACC[hh][:, asl], ALU.add, [pk, 'ACC%d' % hh], ['ACC%d' % hh])

            LA = 2
            for n in range(len(blocks) + LA):
                if n < len(blocks):
                    front(blocks[n])
                m = n - LA
                if m >= 0:
                    back(blocks[m], blocks[m - 1] if m > 0 else None)
            if os.environ.get('P2_FIN', '1') == '0':
                continue
            P.dma('sp', DEN[0:64, :], ACC[0][64:128, :], 'den0', reads=['ACC0'], writes=['DEN'])
            P.dma('sp', DEN[64:128, :], ACC[1][0:64, :], 'den1', reads=['ACC1'], writes=['DEN'])
            for j in range(8):
                js = slice(j * 512, (j + 1) * 512)
                k = j % 2
                P.act(sq[k][0:64, :], ACC[0][0:64, js], AF.Square, ['ACC0'], ['sq%d' % k])
                P.act(sq[k][64:128, :], ACC[1][64:128, js], AF.Square, ['ACC1'], ['sq%d' % k])
                P.mm(psF[:], ones_bd[:], sq[k][:], True, True, ['onesbd', 'sq%d' % k], ['psF'])
                P.stt(d2e[k][:], DEN[:, js], RMS_EPS, DEN[:, js], ALU.mult, ALU.mult, ['DEN'], ['d2e%d' % k])
                P.stt(vv[k][:], psF[:], 1.0 / 64, d2e[k][:], ALU.mult, ALU.add, ['psF', 'd2e%d' % k], ['vv%d' % k])
                P.act(vv[k][:], vv[k][:], AF.Sqrt, ['vv%d' % k], ['vv%d' % k])
                P.recip(vv[k][:], vv[k][:], ['vv%d' % k], ['vv%d' % k])
                P.stt(yTp[0:64, js], ACC[0][0:64, js], gA[0:64, pr:pr + 1], vv[k][0:64, :], ALU.mult, ALU.mult,
                      ['ACC0', 'gA', 'vv%d' % k], ['yTp'])
                P.stt(yTp[64:128, js], ACC[1][64:128, js], gA[64:128, pr:pr + 1], vv[k][64:128, :], ALU.mult, ALU.mult,
                      ['ACC1', 'gA', 'vv%d' % k], ['yTp'])
            P.dma('sp', T_['YTd'][pr, :, :], yTp[:], 'oyT', reads=['yTp'])
        P.emit()


def post_norm_residual(P, psY, keys, gtab, xres, xres_keys, out_tile, out_key, st, tmp, junk, pfx, gkey):
    for hf in range(2):
        P.act(junk[:, hf * 512:(hf + 1) * 512], psY[hf][:], AF.Square, [keys[hf]], ['junk', pfx + 'ss%d' % hf], accum=st[:, hf:hf + 1])
    P.tt('dve', st[:, 2:3], st[:, 0:1], st[:, 1:2], ALU.add, [pfx + 'ss0', pfx + 'ss1'], [pfx + 'sst'])
    P.rstd(st[:, 2:3], st[:, 3:4], st[:, 4:5], 1.0 / D, RMS_EPS, pfx + 'sst')
    for hf in range(2):
        hs = slice(hf * 512, (hf + 1) * 512)
        P.stt(tmp[:, hs], psY[hf][:], st[:, 4:5], gtab[:, hs], ALU.mult, ALU.mult, [keys[hf], pfx + 'sst_r', gkey], [pfx + 'tmp%d' % hf])
        P.tt('pool', out_tile[:, hs], tmp[:, hs], xres[:, hs], ALU.add, [pfx + 'tmp%d' % hf] + list(xres_keys), [out_key])


def pre_norm_T(P, xsrc, xkey, gtab, st, junk, hb, hbkey, psT, ident, hT, tsl, hTkey, pfx, gkey):
    P.act(junk[:], xsrc, AF.Square, [xkey], ['junk', pfx + 'ss'], accum=st[:, 8:9])
    P.rstd(st[:, 8:9], st[:, 9:10], st[:, 10:11], 1.0 / D, RMS_EPS, pfx + 'ss')
    P.stt(hb[:], xsrc, st[:, 10:11], gtab[:], ALU.mult, ALU.mult, [xkey, pfx + 'ss_r', gkey], [hbkey])
    for dc in range(8):
        P.tr(psT[:, dc, :], hb[:, dc * 128:(dc + 1) * 128], ident[:], [hbkey, 'ident'], ['psT'])
    P.cp('act', hT[:, :, tsl], psT[:], ['psT'], [hTkey])


def pass0(P, nc, T_):
    with contextlib.ExitStack() as es:
        sb = lambda n, s, d: es.enter_context(nc.sbuf_tensor(n, s, d))
        ps = lambda n, s, d: es.enter_context(nc.psum_tensor(n, s, d))
        P.begin_pass()
        KmT, Vm = T_['KmT'], T_['Vm']
        ident = sb('p0_id', [128, 128], BF16)
        Wkv = sb('p0_wkv', [128, 8, 2048], BF16)
        gmn = sb('p0_gmn', [128, 1024], F32)
        mhT = sb('p0_mhT', [128, 8, 256], BF16)
        xt = [sb('p0_xt%d' % i, [128, 1024], F32) for i in range(2)]
        junk = sb('p0_junk', [128, 1024], BF16)
        st = sb('p0_st', [128, 16], F32)
        hb = sb('p0_hb', [128, 1024], BF16)
        psY = [ps('p0_psY%d' % i, [128, 512], F32) for i in range(2)]
        psQ = [ps('p0_psQ%d' % i, [128, 512], F32) for i in range(2)]
        psT = ps('p0_psT', [128, 8, 128], BF16)
        P.dma('pool', ident[:], T_['ident'][:, :], 'ident', writes=['ident'], track=False)
        load_w_cols(P, Wkv, T_['w_kv'], 'Wkv', [0, 1, 2, 3])
        P.dma('sp', gmn[:], T_['gt'][:, 4, :], 'g0', writes=['gmn'], track=False)
        for mt in range(2):
            P.dma('sp', xt[mt][:], T_['mem'][mt * 128:(mt + 1) * 128, :], 'xt%d' % mt, writes=['xt%d' % mt])
            pre_norm_T(P, xt[mt][:], 'xt%d' % mt, gmn, st, junk, hb, 'hb', psT, ident, mhT, slice(mt * 128, (mt + 1) * 128), 'mhT', 'm', 'gmn')
        for fc in range(8):
            k = fc % 2
            for dc in range(8):
                P.mm(psQ[k][:, 0:256], Wkv[:, dc, fc * 128:(fc + 1) * 128], mhT[:, dc, :], dc == 0, dc == 7, ['Wkv%d' % (fc // 4), 'mhT'], ['psQ%d' % k])
            P.cp('act', KmT[:, fc, :], psQ[k][:, 0:256], ['psQ%d' % k], ['KmT'])
        for mt in range(2):
            for hf in range(2):
                for dc in range(8):
                    P.mm(psY[hf][:], mhT[:, dc, mt * 128:(mt + 1) * 128], Wkv[:, dc, 1024 + hf * 512:1024 + (hf + 1) * 512], dc == 0, dc == 7,
                         ['Wkv%d' % (2 + hf), 'mhT'], ['psY%d' % hf])
                P.cp('act', Vm[:, mt, hf * 512:(hf + 1) * 512], psY[hf][:], ['psY%d' % hf], ['Vm'])

        P.emit()


def pass3(P, nc, T_, nblk=8):
    x = T_['x']
    with contextlib.ExitStack() as es:
        sb = lambda n, s, d: es.enter_context(nc.sbuf_tensor(n, s, d))
        ps = lambda n, s, d: es.enter_context(nc.psum_tensor(n, s, d))
        P.begin_pass()
        ident = sb('p3_id', [128, 128], BF16)
        ones = sb('p3_ones', [128, 128], BF16)
        Wout = sb('p3_wout', [128, 8, 1024], BF16)
        Wq = sb('p3_wq', [128, 8, 1024], BF16)
        Wo = sb('p3_wo', [128, 8, 1024], BF16)
        gpost = sb('p3_gpost', [128, 1024], F32)
        gpm = sb('p3_gpm', [128, 1024], F32)
        gpostm = sb('p3_gpostm', [128, 1024], F32)
        yTb = [sb('p3_yTb%d' % i, [128, 8, 512], BF16) for i in range(2)]
        xt = [sb('p3_xt%d' % i, [128, 1024], F32) for i in range(2)]
        x1 = sb('p3_x1', [128, 4, 1024], F32)
        x2 = [sb('p3_x2%d' % i, [128, 1024], F32) for i in range(2)]
        tmp = sb('p3_tmp', [128, 1024], F32)
        junk = sb('p3_junk', [128, 1024], BF16)
        st = sb('p3_st', [128, 16], F32)
        hb = sb('p3_hb', [128, 1024], BF16)
        h2T = sb('p3_h2T', [128, 8, 512], BF16)
        qT = sb('p3_qT', [128, 8, 512], BF16)
        PTm = sb('p3_PTm', [128, 8, 512], BF16)
        oT = sb('p3_oT', [128, 8, 512], BF16)
        rden = [sb('p3_rden%d' % i, [128, 512], F32) for i in range(2)]
        psY = [ps('p3_psY%d' % i, [128, 512], F32) for i in range(2)]
        psT = ps('p3_psT', [128, 8, 128], BF16)
        psQ = [ps('p3_psQ%d' % i, [128, 512], F32) for i in range(2)]
        psO = [ps('p3_psO%d' % i, [128, 512], F32) for i in range(2)]
        psD = ps('p3_psD', [128, 512], F32)

        P.dma('pool', ident[:], T_['ident'][:, :], 'ident', writes=['ident'], track=False)
        load_w_cols(P, Wout, T_['w_out'], 'Wout', [0, 1])
        load_w_cols(P, Wq, T_['w_q'], 'Wq', [0, 1])
        load_w_cols(P, Wo, T_['w_o'], 'Wo', [0, 1])
        P.memset('pool', ones[:], 1.0, ['ones'])
        KmT, Vm = T_['KmT'], T_['Vm']
        for k, (t, gi) in enumerate(((gpost, 1), (gpm, 2), (gpostm, 3))):
            P.dma('sp', t[:], T_['gt'][:, gi, :], 'g%d' % k, writes=['g%d' % gi], track=False)

        for blk in range(nblk):
            bi = blk % 2
            bs = slice(blk * 512, (blk + 1) * 512)
            P.dma('sp', yTb[bi][:], T_['YTd'][:, :, bs].rearrange("a p n -> p a n"), 'yTb%d' % bi, writes=['yTb%d' % bi])
            for t in range(4):
                T = 4 * blk + t
                i = T % 2
                tsl = slice(t * 128, (t + 1) * 128)
                P.dma('sp', xt[i][:], x[T * 128:(T + 1) * 128, :], 'xt%d' % i, writes=['xt%d' % i])
                for hf in range(2):
                    for fc in range(8):
                        P.mm(psY[hf][:], yTb[bi][:, fc, tsl], Wout[:, fc, hf * 512:(hf + 1) * 512], fc == 0, fc == 7,
                             ['yTb%d' % bi, 'Wout%d' % hf], ['psY%d' % hf])
                post_norm_residual(P, psY, ['psY0', 'psY1'], gpost, xt[i], ['xt%d' % i], x1[:, t, :], 'x1_%d' % t, st, tmp, junk, 'a', 'g1')
                pre_norm_T(P, x1[:, t, :], 'x1_%d' % t, gpm, st, junk, hb, 'hb', psT, ident, h2T, tsl, 'h2T', 'b', 'g2')
            for fc in range(8):
                k = fc % 2
                for dc in range(8):
                    P.mm(psQ[k][:], Wq[:, dc, fc * 128:(fc + 1) * 128], h2T[:, dc, :], dc == 0, dc == 7, ['Wq%d' % (fc // 4), 'h2T'], ['psQ%d' % k])
                P.cp('act', qT[:, fc, :], psQ[k][:], ['psQ%d' % k], ['qT%d' % fc])
            for h in range(4):
                for mt in range(2):
                    k = (h * 2 + mt) % 2
                    for cc in range(2):
                        P.mm(psQ[k][:], KmT[:, 2 * h + cc, mt * 128:(mt + 1) * 128], qT[:, 2 * h + cc, :], cc == 0, cc == 1,
                             ['KmT', 'qT%d' % (2 * h + cc)], ['psQ%d' % k])
                    P.act(PTm[:, h * 2 + mt, :], psQ[k][:], AF.Exp, ['psQ%d' % k], ['PTm%d' % h], scale=1.0 / 16)
            for h in range(4):
                for mt in range(2):
                    P.mm(psD[:], ones[:], PTm[:, h * 2 + mt, :], mt == 0, mt == 1, ['ones', 'PTm%d' % h], ['psD'])
                P.recip(rden[h % 2][:], psD[:], ['psD'], ['rden%d' % (h % 2)])
                for cc in range(2):
                    fcx = 2 * h + cc
                    for mt in range(2):
                        P.mm(psO[cc][:], Vm[:, mt, fcx * 128:(fcx + 1) * 128], PTm[:, h * 2 + mt, :], mt == 0, mt == 1,
                             ['Vm', 'PTm%d' % h], ['psO%d' % cc])
                    P.tt('dve', oT[:, fcx, :], psO[cc][:], rden[h % 2][:], ALU.mult, ['psO%d' % cc, 'rden%d' % (h % 2)], ['oT'])
            for t in range(4):
                T = 4 * blk + t
                i = T % 2
                tsl = slice(t * 128, (t + 1) * 128)
                for hf in range(2):
                    for fc in range(8):
                        P.mm(psY[hf][:], oT[:, fc, tsl], Wo[:, fc, hf * 512:(hf + 1) * 512], fc == 0, fc == 7, ['oT', 'Wo%d' % hf], ['psY%d' % hf])
                post_norm_residual(P, psY, ['psY0', 'psY1'], gpostm, x1[:, t, :], ['x1_%d' % t], x2[i], 'x2_%d' % i, st, tmp, junk, 'c', 'g3')
                P.dma('sp', T_['X2d'][T * 128:(T + 1) * 128, :], x2[i][:], 'ox2_%d' % i, reads=['x2_%d' % i])
        P.emit()


def pass4(P, nc, T_, nblk=8):
    with contextlib.ExitStack() as es:
        sb = lambda n, s, d: es.enter_context(nc.sbuf_tensor(n, s, d))
        ps = lambda n, s, d: es.enter_context(nc.psum_tensor(n, s, d))
        P.begin_pass()
        ident = sb('p4_id', [128, 128], BF16)
        Wgu = sb('p4_wgu', [128, 8, 2 * FFN], BF16)
        Wd = sb('p4_wd', [128, NFC, 1024], BF16)
        gpf = sb('p4_gpf', [128, 1024], F32)
        gpostf = sb('p4_gpostf', [128, 1024], F32)
        xt = [sb('p4_xt%d' % i, [128, 1024], F32) for i in range(2)]
        ot = [sb('p4_ot%d' % i, [128, 1024], F32) for i in range(2)]
        tmp = sb('p4_tmp', [128, 1024], F32)
        junk = sb('p4_junk', [128, 1024], BF16)
        st = sb('p4_st', [128, 16], F32)
        hb = sb('p4_hb', [128, 1024], BF16)
        h3T = [sb('p4_h3T%d' % i, [128, 8, 512], BF16) for i in range(2)]
        actT = sb('p4_actT', [128, NFC, 512], BF16)
        sg = [sb('p4_sg%d' % i, [128, 512], F32) for i in range(2)]
        psY = [ps('p4_psY%d' % i, [128, 512], F32) for i in range(2)]
        psT = ps('p4_psT', [128, 8, 128], BF16)
        psG = [ps('p4_psG%d' % i, [128, 512], F32) for i in range(2)]
        psU = [ps('p4_psU%d' % i, [128, 512], F32) for i in range(2)]

        P.dma('pool', ident[:], T_['ident'][:, :], 'ident', writes=['ident'], track=False)
        load_w_cols(P, Wgu, T_['w_gu'], 'Wgu', [0, 5, 6, 1, 7, 2, 8, 3, 9, 4, 10])
        load_w(P, Wd, T_['w_dn'], NFC, 'Wd')
        P.dma('sp', gpf[:], T_['gt'][:, 5, :], 'g0', writes=['gpf'], track=False)
        P.dma('sp', gpostf[:], T_['gt'][:, 6, :], 'g1', writes=['gpostf'], track=False)

        def xload(T):
            i = T % 2
            P.dma('sp', xt[i][:], T_['X2d'][T * 128:(T + 1) * 128, :], 'xt%d' % i, writes=['xt%d' % i])

        def chain(T):
            i = T % 2
            P.act(junk[:], xt[i][:], AF.Square, ['xt%d' % i], ['junk', 'bss'], accum=st[:, 8:9])
            P.rstd(st[:, 8:9], st[:, 9:10], st[:, 10:11], 1.0 / D, RMS_EPS, 'bss')
            P.stt(hb[:], xt[i][:], st[:, 10:11], gpf[:], ALU.mult, ALU.mult, ['xt%d' % i, 'bss_r', 'gpf'], ['hb'])

        def trans(T):
            buf, t = (T // 4) % 2, T % 4
            for dc in range(8):
                P.tr(psT[:, dc, :], hb[:, dc * 128:(dc + 1) * 128], ident[:], ['hb', 'ident'], ['psT'])
            P.cp('act', h3T[buf][:, :, t * 128:(t + 1) * 128], psT[:], ['psT'], ['h3T%d' % buf])

        xload(0)
        for t in range(4):
            if t + 1 < 4:
                xload(t + 1)
            chain(t)
            trans(t)
        for blk in range(nblk):
            hcur = h3T[blk % 2]
            hk = 'h3T%d' % (blk % 2)
            nxt = blk + 1 < nblk
            if nxt:
                xload(4 * (blk + 1))
            for j in range(NFC):
                k = j % 2
                for dc in range(8):
                    P.mm(psG[k][:], Wgu[:, dc, j * 128:(j + 1) * 128], hcur[:, dc, :], dc == 0, dc == 7, ['Wgu%d' % (j * 128 // 512), hk], ['psG%d' % k])
                for dc in range(8):
                    P.mm(psU[k][:], Wgu[:, dc, FFN + j * 128:FFN + (j + 1) * 128], hcur[:, dc, :], dc == 0, dc == 7, ['Wgu%d' % ((FFN + j * 128) // 512), hk], ['psU%d' % k])
                P.act(sg[k][:], psG[k][:], AF.Silu, ['psG%d' % k], ['sg%d' % k])
                P.tt('dve', actT[:, j, :], psU[k][:], sg[k][:], ALU.mult, ['psU%d' % k, 'sg%d' % k], ['actT'])
                if nxt and j >= 1 and (j - 1) % 4 == 0 and (j - 1) // 4 < 4:
                    t = (j - 1) // 4
                    if t + 1 < 4:
                        xload(4 * (blk + 1) + t + 1)
                    chain(4 * (blk + 1) + t)
                if nxt and j >= 4 and j % 4 == 0 and j // 4 <= 4:
                    trans(4 * (blk + 1) + j // 4 - 1)
            for t in range(4):
                T = 4 * blk + t
                i = T % 2
                tsl = slice(t * 128, (t + 1) * 128)
                P.dma('sp', ot[i][:], T_['X2d'][T * 128:(T + 1) * 128, :], 'otl%d' % i, writes=['ot%d' % i])
                for hf in range(2):
                    for fc in range(NFC):
                        P.mm(psY[hf][:], actT[:, fc, tsl], Wd[:, fc, hf * 512:(hf + 1) * 512], fc == 0, fc == NFC - 1, ['actT', 'Wd'], ['psY%d' % hf])
                post_norm_residual(P, psY, ['psY0', 'psY1'], gpostf, ot[i], ['ot%d' % i], ot[i], 'ot%d' % i, st, tmp, junk, 'c', 'gpostf')
                P.dma('sp', T_['out'][T * 128:(T + 1) * 128, :], ot[i][:], 'oo%d' % i, reads=['ot%d' % i])
        P.emit()


def host_tables():
    pos = np.arange(S, dtype=np.float32)
    invA = (np.float32(500000.0) ** (-(np.arange(0, 16, 2, dtype=np.float32) / np.float32(16)))).astype(np.float32)
    angA = (pos[:, None] * invA[None, :]).astype(np.float32).astype(np.float64)
    invR = (np.float32(10000.0) ** (-(np.arange(0, 128, 2, dtype=np.float32) / np.float32(128)))).astype(np.float32)
    angR = (pos[:, None] * invR[None, :]).astype(np.float32).astype(np.float64)

    def lay(a):
        return np.ascontiguousarray(a.reshape(NT, 128, -1).transpose(1, 0, 2)).astype(np.float32)
    tabs = {'cosA': lay(np.cos(angA)), 'sinA': lay(np.sin(angA)), 'cosR': lay(np.cos(angR)), 'sinR': lay(np.sin(angR))}
    H = 4
    log_g = np.log(1.0 - 2.0 ** (-5.0 - np.arange(H, dtype=np.float64)))
    n = np.arange(128, dtype=np.float64)
    sc = 128.0 ** -0.5
    decT = np.zeros((128, H, 128), np.float64)
    for h in range(H):
        rel = n[None, :] - n[:, None]
        decT[:, h, :] = np.where(rel >= 0, np.exp(log_g[h] * np.maximum(rel, 0.0)), 0.0) * sc
    xi = np.exp(log_g[:, None] * (n[None, :] + 1.0))
    zeta = np.exp(log_g[:, None] * (127.0 - n[None, :])) * sc
    tabs['decT'] = decT.astype(np.float32)
    tabs['xiT'] = np.ascontiguousarray(np.broadcast_to(xi[None, :, :], (128, H, 128))).astype(np.float32)
    tabs['zet'] = np.ascontiguousarray(np.broadcast_to(zeta.T[:, :, None], (128, H, 128))).astype(np.float32)
    tabs['cd'] = np.exp(log_g * 128.0)
    tabs['ident'] = np.eye(128, dtype=np.float32)
    j = np.arange(128)[:, None]
    i = np.arange(128)[None, :]
    cur = (j <= i).astype(np.float32)
    prv = (j >= i).astype(np.float32)
    tabs['mask'] = np.concatenate([cur, prv, cur, prv], axis=1)
    bd = np.zeros((128, 128), np.float32)
    bd[0:64, 0:64] = 1.0
    bd[64:128, 64:128] = 1.0
    tabs['onesbd'] = bd
    return tabs


def build(stages=(0, 1, 2, 3, 4), dbg=False, tabs=None):
    nc = bass.Bass("TRN2", target_bir_lowering=False)
    T_ = {}

    def inp(name, shape):
        T_[name] = nc.dram_tensor(name, list(shape), F32, kind="ExternalInput").ap()
    inp('x', [S, D])
    inp('mem', [256, D])
    inp('w_in', [D, 3584])
    inp('w_out', [D, D])
    inp('w_q', [D, D])
    inp('w_kv', [D, 2 * D])
    inp('w_o', [D, D])
    inp('w_gu', [D, 2 * FFN])
    inp('w_dn', [FFN, D])
    inp('gt', [128, 10, D])
    inp('gA', [128, 4])
    for nm in ('cosA', 'sinA'):
        inp(nm, [128, 32, 8])
    for nm in ('cosR', 'sinR'):
        inp(nm, [128, 32, 64])
    for nm in ('decT', 'xiT', 'zet'):
        inp(nm, [128, 4, 128])
    inp('ident', [128, 128])
    inp('mask', [128, 512])
    inp('onesbd', [128, 128])
    T_['cd'] = tabs['cd']
    T_['out'] = nc.dram_tensor('out', [S, D], F32, kind="ExternalOutput").ap()
    kind = "ExternalOutput" if dbg else "Internal"
    T_['QTd'] = nc.dram_tensor('QTd', [4, 128, S], BF16, kind=kind).ap()
    T_['KTd'] = nc.dram_tensor('KTd', [4, 128, S], BF16, kind=kind).ap()
    T_['VTd'] = nc.dram_tensor('VTd', [4, 128, S], BF16, kind=kind).ap()
    T_['YTd'] = nc.dram_tensor('YTd', [8, 128, S], BF16, kind=kind).ap()
    T_['X2d'] = nc.dram_tensor('X2d', [S, D], F32, kind=kind).ap()
    with contextlib.ExitStack() as es:
        P = Prog(nc, es)
        with contextlib.ExitStack() as es2:
            T_['KmT'] = es2.enter_context(nc.sbuf_tensor('KmT', [128, 8, 256], BF16))
            T_['Vm'] = es2.enter_context(nc.sbuf_tensor('Vm', [128, 2, 1024], BF16))
            if 0 in stages:
                pass0(P, nc, T_)
            if 1 in stages:
                pass1(P, nc, T_)
            if 2 in stages:
                pass2(P, nc, T_)
            if 3 in stages:
                pass3(P, nc, T_)
        if 4 in stages:
            pass4(P, nc, T_)
    return nc


def make_inputs(x, mem, pre_mix_g, post_mix_g, w_in, attn_gn_g, ret_gn_g, w_out,
                pre_mem_g, post_mem_g, mem_norm_g, w_q_mem, w_kv_mem, w_o_mem,
                pre_ffn_g, post_ffn_g, w_gate_up, w_down, tabs):
    f = lambda a: np.ascontiguousarray(np.asarray(a, dtype=np.float32))
    gt = np.zeros((128, 10, D), np.float32)
    for k, g in enumerate((pre_mix_g, post_mix_g, pre_mem_g, post_mem_g, mem_norm_g, pre_ffn_g, post_ffn_g)):
        gt[:, k, :] = np.broadcast_to(f(g)[0][None, :], (128, D))
    gt[:, 9, 0:512] = np.broadcast_to(f(ret_gn_g)[0][None, :], (128, 512))
    gA = np.ascontiguousarray(f(attn_gn_g)[0].reshape(4, 128).T)
    shared = {
        'w_in': f(w_in)[0], 'w_out': f(w_out)[0], 'w_q': f(w_q_mem)[0], 'w_kv': f(w_kv_mem)[0], 'w_o': f(w_o_mem)[0],
        'w_gu': f(w_gate_up)[0], 'w_dn': f(w_down)[0], 'gt': gt, 'gA': gA,
    }
    for nm in ('cosA', 'sinA', 'cosR', 'sinR', 'decT', 'xiT', 'zet', 'ident', 'mask', 'onesbd'):
        shared[nm] = tabs[nm]
    xs = f(x)
    ms = f(mem)
    return [dict(shared, x=xs[b], mem=ms[b]) for b in range(xs.shape[0])]


def kernel(**inputs):
    tabs = host_tables()
    in_maps = make_inputs(tabs=tabs, **inputs)
    nc = build(tabs=tabs)
    res = run_bass_kernel_spmd(nc, in_maps, core_ids=list(range(8)))
    return np.stack([np.asarray(r['out'], dtype=np.float32) for r in res.results], axis=0)
```
